# Optimizing a Trainium2 kernel written in Bass

```python
import math
import jax, jax.numpy as jnp
from jax import lax
import numpy as np

D_MODEL = 1024
BATCH = 1
SEQ = 16384
DEPTH = 1

MIX_WIDTH = D_MODEL
DIFF_WIDTH = MIX_WIDTH // 2
RET_WIDTH = MIX_WIDTH - DIFF_WIDTH
N_DIFF_HEADS = 4
DIFF_HEAD_DIM = DIFF_WIDTH // N_DIFF_HEADS // 2
DIFF_V_DIM = 2 * DIFF_HEAD_DIM
N_RET_HEADS = 4
RET_HEAD_DIM = RET_WIDTH // N_RET_HEADS
D_FF = 4 * D_MODEL
Q_BLOCK = 128
RET_CHUNK = 128
NORM_EPS = 1e-5
IN_WIDTH = 3 * DIFF_WIDTH + 4 * RET_WIDTH

kernel_name = "hybrid_diffattn_retention_block"


def rms_norm(x, g, eps=NORM_EPS):
    xf = x.astype(jnp.float32)
    y = xf * lax.rsqrt(jnp.mean(xf * xf, axis=-1, keepdims=True) + eps)
    return (y * g.astype(jnp.float32)).astype(x.dtype)


def alibi_slopes(n_heads):
    return jnp.exp2(-8.0 * jnp.arange(1, n_heads + 1, dtype=jnp.float32) / n_heads)


def diff_attention(q, k, v, lam):
    b, s = q.shape[0], q.shape[1]
    nb = s // Q_BLOCK
    scale = DIFF_HEAD_DIM ** -0.5
    slopes = alibi_slopes(N_DIFF_HEADS)
    kpos = jnp.arange(s, dtype=jnp.float32)
    qb = (q * scale).reshape(b, nb, Q_BLOCK, N_DIFF_HEADS, 2, DIFF_HEAD_DIM)
    qb = qb.transpose(1, 0, 3, 4, 2, 5)
    kt = k.transpose(0, 2, 3, 1, 4)
    vt = v.transpose(0, 2, 1, 3)

    def block(args):
        bi, qblk = args
        qpos = (bi * Q_BLOCK + jnp.arange(Q_BLOCK)).astype(jnp.float32)
        dist = qpos[:, None] - kpos[None, :]
        bias = jnp.where(dist >= 0, -slopes[:, None, None] * dist, -jnp.inf)
        scores = jnp.einsum('bhmqd,bhmkd->bhmqk', qblk, kt) + bias[None, :, None]
        p = jax.nn.softmax(scores, axis=-1)
        a = p[:, :, 0] - lam * p[:, :, 1]
        return jnp.einsum('bhqk,bhkv->bhqv', a, vt)

    out = lax.map(block, (jnp.arange(nb), qb))
    return out.transpose(1, 0, 3, 2, 4).reshape(b, s, N_DIFF_HEADS, DIFF_V_DIM)


def retention_chunkwise(q, k, v):
    b, s = q.shape[0], q.shape[1]
    C = RET_CHUNK
    nc = s // C
    log_g = jnp.log1p(-jnp.exp2(-5.0 - jnp.arange(N_RET_HEADS, dtype=jnp.float32)))
    i = jnp.arange(C, dtype=jnp.float32)
    rel = i[:, None] - i[None, :]
    d_intra = jnp.where(rel >= 0, jnp.exp(log_g[:, None, None] * jnp.maximum(rel, 0.0)), 0.0)
    q_decay = jnp.exp(log_g[:, None] * (i + 1.0))[None, :, :, None]
    k_decay = jnp.exp(log_g[:, None] * (C - 1.0 - i))[None, :, :, None]
    chunk_decay = jnp.exp(log_g * C)[None, :, None, None]
    k = k * (RET_HEAD_DIM ** -0.5)

    def to_chunks(t):
        return t.reshape(b, nc, C, N_RET_HEADS, t.shape[-1]).transpose(1, 0, 3, 2, 4)

    def step(state, inp):
        qc, kc, vc = inp
        intra = jnp.einsum('bhqd,bhkd->bhqk', qc, kc) * d_intra[None]
        o = (jnp.einsum('bhqk,bhkv->bhqv', intra, vc)
             + jnp.einsum('bhqd,bhdv->bhqv', qc, state) * q_decay)
        state = state * chunk_decay + jnp.einsum('bhkd,bhkv->bhdv', kc * k_decay, vc)
        return state, o

    state0 = jnp.zeros((b, N_RET_HEADS, RET_HEAD_DIM, RET_HEAD_DIM), jnp.float32)
    _, o = lax.scan(step, state0, (to_chunks(q), to_chunks(k), to_chunks(v)))
    return o.transpose(1, 0, 3, 2, 4).reshape(b, s, N_RET_HEADS, RET_HEAD_DIM)


def setup_inputs(seed: int = 0) -> dict:
    key = jax.random.key(seed)
    ks = jax.random.split(key, 16)
    f32 = jnp.float32

    def nrm(k, shape, scale):
        return jax.random.normal(k, shape, f32) * scale

    return {
        "x": nrm(ks[0], (BATCH, SEQ, D_MODEL), 1.0),
        "norm1_g": 1.0 + nrm(ks[1], (DEPTH, D_MODEL), 0.02),
        "w_in": nrm(ks[2], (DEPTH, D_MODEL, IN_WIDTH), D_MODEL ** -0.5),
        "lambda_q1": nrm(ks[3], (DEPTH, DIFF_HEAD_DIM), 0.1),
        "lambda_k1": nrm(ks[4], (DEPTH, DIFF_HEAD_DIM), 0.1),
        "lambda_q2": nrm(ks[5], (DEPTH, DIFF_HEAD_DIM), 0.1),
        "lambda_k2": nrm(ks[6], (DEPTH, DIFF_HEAD_DIM), 0.1),
        "diff_norm_g": 1.0 + nrm(ks[7], (DEPTH, DIFF_V_DIM), 0.02),
        "ret_norm_g": 1.0 + nrm(ks[8], (DEPTH, RET_WIDTH), 0.02),
        "w_out": nrm(ks[9], (DEPTH, MIX_WIDTH, D_MODEL), MIX_WIDTH ** -0.5),
        "norm2_g": 1.0 + nrm(ks[10], (DEPTH, D_MODEL), 0.02),
        "w_up": nrm(ks[11], (DEPTH, D_MODEL, D_FF), D_MODEL ** -0.5),
        "w_down": nrm(ks[12], (DEPTH, D_FF, D_MODEL), D_FF ** -0.5),
        "final_norm_g": 1.0 + nrm(ks[13], (D_MODEL,), 0.02),
    }


def reference(x, norm1_g, w_in, lambda_q1, lambda_k1, lambda_q2, lambda_k2,
              diff_norm_g, ret_norm_g, w_out, norm2_g, w_up, w_down, final_norm_g):
    b, s = x.shape[0], x.shape[1]
    f32 = jnp.float32
    splits = [DIFF_WIDTH, 2 * DIFF_WIDTH, 3 * DIFF_WIDTH,
              3 * DIFF_WIDTH + RET_WIDTH, 3 * DIFF_WIDTH + 2 * RET_WIDTH,
              3 * DIFF_WIDTH + 3 * RET_WIDTH]
    for l in range(DEPTH):
        lambda_init = 0.8 - 0.6 * math.exp(-0.3 * l)
        h = rms_norm(x, norm1_g[l])
        proj = jnp.einsum('bsd,de->bse', h, w_in[l]).astype(f32)
        dq, dk, dv, rq, rk, rv, rg = jnp.split(proj, splits, axis=-1)

        lam = (jnp.exp(jnp.sum(lambda_q1[l].astype(f32) * lambda_k1[l].astype(f32)))
               - jnp.exp(jnp.sum(lambda_q2[l].astype(f32) * lambda_k2[l].astype(f32)))
               + lambda_init)
        dq = dq.reshape(b, s, N_DIFF_HEADS, 2, DIFF_HEAD_DIM)
        dk = dk.reshape(b, s, N_DIFF_HEADS, 2, DIFF_HEAD_DIM)
        dv = dv.reshape(b, s, N_DIFF_HEADS, DIFF_V_DIM)
        a_out = diff_attention(dq, dk, dv, lam)
        a_out = rms_norm(a_out, diff_norm_g[l]) * (1.0 - lambda_init)
        a_out = a_out.reshape(b, s, DIFF_WIDTH)

        rq = rq.reshape(b, s, N_RET_HEADS, RET_HEAD_DIM)
        rk = rk.reshape(b, s, N_RET_HEADS, RET_HEAD_DIM)
        rv = rv.reshape(b, s, N_RET_HEADS, RET_HEAD_DIM)
        r_out = retention_chunkwise(rq, rk, rv)
        r_out = rms_norm(r_out, ret_norm_g[l].reshape(N_RET_HEADS, RET_HEAD_DIM))
        r_out = r_out.reshape(b, s, RET_WIDTH) * jax.nn.silu(rg)

        mix = jnp.concatenate([a_out, r_out], axis=-1).astype(x.dtype)
        x = x + jnp.einsum('bse,ed->bsd', mix, w_out[l])

        h2 = rms_norm(x, norm2_g[l])
        up = jnp.square(jax.nn.relu(jnp.einsum('bsd,df->bsf', h2, w_up[l])))
        x = x + jnp.einsum('bsf,fd->bsd', up, w_down[l])
    return rms_norm(x, final_norm_g)
```

```python
import math
import numpy as np
import ml_dtypes
import concourse.bass as bass
import concourse.mybir as mybir
from concourse.bass_utils import run_bass_kernel_spmd

F32 = mybir.dt.float32
BF16 = mybir.dt.bfloat16
AF = mybir.ActivationFunctionType
ALU = mybir.AluOpType

NCORES = 8
S = 16384
D = 1024
DFF = 4096
NSLOT = 4
OWN = 2048
INW = 3584
EPS = 1e-5
LAMBDA_INIT = 0.8 - 0.6 * math.exp(-0.3 * 0)
NEG = -30000.0
SLOPES = [2.0 ** (-2.0 * (h + 1)) for h in range(4)]
NCOEF = 40
VW = 130


class _Rec:
    def __init__(self):
        self.call = None

    def __getattr__(self, name):
        def f(*a, **kw):
            assert self.call is None
            self.call = (name, a, kw)
            return self
        return f

    def replay(self, e):
        name, a, kw = self.call
        return getattr(e, name)(*a, **kw)


def _record_call(fn):
    r = _Rec()
    fn(r)
    assert r.call is not None
    return r.replay


class Sched:
    ENGS = ["pe", "act", "dve", "pool", "sp"]

    def __init__(self):
        self.ops = []
        self.lastw = {}
        self.rd_eng = {}
        self.rd_dma = {}
        self.last_of_eng = {}
        self.dmas_since_barrier = []

    def _deps(self, R, W):
        deps = set()
        for r in R:
            w = self.lastw.get(r)
            if w is not None:
                deps.add(w)
        for w_ in W:
            w = self.lastw.get(w_)
            if w is not None:
                deps.add(w)
            for v in self.rd_eng.get(w_, {}).values():
                deps.add(v)
            for v in self.rd_dma.get(w_, ()):
                deps.add(v)
        return deps

    def _record(self, idx, eng, R, W, is_dma):
        for r in R:
            if is_dma:
                self.rd_dma.setdefault(r, []).append(idx)
            else:
                self.rd_eng.setdefault(r, {})[eng] = idx
        for w_ in W:
            self.lastw[w_] = idx
            self.rd_eng[w_] = {}
            self.rd_dma[w_] = []
        self.last_of_eng[eng] = idx

    def op(self, eng, fn, R=(), W=()):
        idx = len(self.ops)
        if eng != "pe":
            W = list(W) + [("psr", r[1]) for r in R if isinstance(r, tuple) and r[0] == "ps"]
        deps = self._deps(R, W)
        self.ops.append(dict(eng=eng, fn=_record_call(fn), deps=deps, dma=False))
        self._record(idx, eng, R, W, False)
        return idx

    def dma(self, eng, fns, R=(), W=(), key=None):
        idx = len(self.ops)
        deps = self._deps(R, W)
        if not isinstance(fns, (list, tuple)):
            fns = [fns]
        self.ops.append(dict(eng=eng, fn=[_record_call(f) for f in fns], deps=deps, dma=True, key=key))
        self._record(idx, eng, R, W, True)
        self.dmas_since_barrier.append(idx)
        return idx

    def barrier(self):
        deps = set(self.last_of_eng.values()) | set(self.dmas_since_barrier)
        self.dmas_since_barrier = []
        for eng in self.ENGS:
            idx = len(self.ops)
            self.ops.append(dict(eng=eng, fn=None, deps=set(deps), dma=False))
            self.last_of_eng[eng] = idx

    def finalize(self, nc, semctx):
        needed = set()
        for i, o in enumerate(self.ops):
            for d in o["deps"]:
                od = self.ops[d]
                if od["eng"] == "pe" and o["eng"] == "pe" and not od["dma"] and not o["dma"]:
                    continue
                needed.add(d)
        self.eng_sem = {e: semctx("sem_" + e) for e in self.ENGS}
        self.key_sem = {}
        cnt = {e: 0 for e in self.ENGS}
        keycnt = {}
        for i, o in enumerate(self.ops):
            if o["dma"]:
                k = o["key"]
                if k not in self.key_sem:
                    self.key_sem[k] = semctx("dsem_%d" % len(self.key_sem))
                keycnt[k] = keycnt.get(k, 0) + len(o["fn"])
                o["sig"] = (self.key_sem[k], 16 * keycnt[k])
            elif o["fn"] is not None and i in needed:
                cnt[o["eng"]] += 1
                o["sig"] = (self.eng_sem[o["eng"]], cnt[o["eng"]])
                o["inc"] = True
            elif o["fn"] is None:
                o["sig"] = None
        self.nsem = len(self.key_sem) + len(self.ENGS)

    def run(self, engname, e):
        waited = {}
        for o in self.ops:
            if o["eng"] != engname:
                continue
            waits = {}
            for d in o["deps"]:
                od = self.ops[d]
                if od["eng"] == "pe" and engname == "pe" and not od["dma"] and not o["dma"]:
                    continue
                sig = od.get("sig")
                if sig is None:
                    continue
                sem, val = sig
                k = id(sem)
                if waited.get(k, (None, 0))[1] >= val:
                    continue
                if k not in waits or waits[k][1] < val:
                    waits[k] = (sem, val)
            for k, (sem, val) in waits.items():
                e.wait_ge(sem, val)
                waited[k] = (sem, val)
            if o["fn"] is None:
                continue
            if o["dma"]:
                sem, _ = o["sig"]
                for f in o["fn"]:
                    f(e).then_inc(sem, 16)
            else:
                ins = o["fn"](e)
                if o.get("inc"):
                    ins.then_inc(o["sig"][0], 1)


class Arena:
    def __init__(self, tensor, nbytes):
        self.t = tensor
        self.nbytes = nbytes
        self.off = 0

    def alloc(self, free_shape, dtype):
        esz = 4 if dtype == F32 else 2
        n = 1
        for s in free_shape:
            n *= s
        nb = n * esz
        self.off = (self.off + 63) // 64 * 64
        o = self.off
        self.off += nb
        assert self.off <= self.nbytes, ("SBUF arena overflow", self.off, self.nbytes)
        v = self.t[:, o // 2:(o + nb) // 2]
        if dtype == F32:
            v = v.bitcast(F32)
        if len(free_shape) == 2:
            v = v.rearrange("p (a b) -> p a b", a=free_shape[0])
        elif len(free_shape) == 3:
            v = v.rearrange("p (a b c) -> p a b c", a=free_shape[0], b=free_shape[1])
        return v

    def mark(self):
        return self.off

    def reset(self, m):
        self.off = m


def build_program(debug=False, stop=None):
    nc = bass.Bass("TRN2", target_bir_lowering=False)
    sch = Sched()

    def din(name, shape, dt=F32):
        return nc.dram_tensor(name, list(shape), dt, kind="ExternalInput").ap()

    xall = din("xall", [S, D])
    xown = din("xown", [OWN, D])
    w_in = din("w_in", [D, INW])
    w_out = din("w_out", [D, D])
    w_up = din("w_up", [D, DFF])
    w_down = din("w_down", [DFF, D])
    g1_d = din("g1", [128, D])
    g2_d = din("g2", [128, D])
    gf_d = din("gf", [128, D])
    gdiff_d = din("gdiff", [128, 128])
    gret_d = din("gret", [128, 512])
    lamv_d = din("lamv", [128, 256])
    kaug_d = din("kaug", [36, S], BF16)
    qaug_d = din("qaug", [4, 36, OWN], BF16)
    kaugo_d = din("kaugo", [4, OWN], BF16)
    cmask_d = din("cmask", [128, 4, 512], BF16)
    identf_d = din("identf", [128, 128])
    identb_d = din("identb", [128, 128], BF16)
    dm_d = din("dm", [128, 4, 512])
    qdec_d = din("qdec", [128, 4, 512])
    kdecp_d = din("kdecp", [128, 16])
    rcoef_d = din("rcoef", [128, NCOEF])
    out_d = nc.dram_tensor("out", [OWN, D], F32, kind="ExternalOutput").ap()
    if debug:
        dbg_mix = nc.dram_tensor("dbg_mix", [OWN, D], BF16, kind="ExternalOutput").ap()
        dbg_kt = nc.dram_tensor("dbg_kt", [8, 64, 2048], BF16, kind="ExternalOutput").ap()
        dbg_v = nc.dram_tensor("dbg_v", [128, 4, 16, VW], BF16, kind="ExternalOutput").ap()
        dbg_qt = nc.dram_tensor("dbg_qt", [8, 64, OWN], BF16, kind="ExternalOutput").ap()
        dbg_sown = nc.dram_tensor("dbg_sown", [128, 2048], F32, kind="ExternalOutput").ap()
        dbg_accs = nc.dram_tensor("dbg_accs", [128, 8 * VW], F32, kind="ExternalOutput").ap()
        dbg_e = nc.dram_tensor("dbg_e", [128, 512], BF16, kind="ExternalOutput").ap()
        dbg_s = nc.dram_tensor("dbg_s", [128, 512], F32, kind="ExternalOutput").ap()
        dbg_kq = nc.dram_tensor("dbg_kq", [128, 1024], BF16, kind="ExternalOutput").ap()
    Sown_keep = []

    KTs = nc.dram_tensor("KTs", [8, 64, S], BF16).ap()
    Vs = nc.dram_tensor("Vs", [128, 4, 128, VW], BF16).ap()
    QTs = nc.dram_tensor("QTs", [8, 64, OWN], BF16).ap()
    KTo = nc.dram_tensor("KTo", [8, 64, OWN], BF16).ap()
    Vo = nc.dram_tensor("Vo", [128, 4, 16, VW], BF16).ap()
    Wup_s = nc.dram_tensor("Wup_s", [8, 128, 8, 512], BF16).ap()
    Wdn_s = nc.dram_tensor("Wdn_s", [8, 128, 4, 1024], BF16).ap()

    ARENA_BYTES = 206 * 1024
    ctx_arena = nc.sbuf_tensor("arena", [128, ARENA_BYTES // 2], BF16)
    arena_t = ctx_arena.__enter__()
    ar = Arena(arena_t, ARENA_BYTES)
    banks = []
    bank_ctx = []
    for b in range(8):
        cx = nc.psum_tensor("psb%d" % b, [128, 512], F32)
        bank_ctx.append(cx)
        banks.append(cx.__enter__())

    bank_rr = [0]

    def next_bank(lo=0, hi=8):
        b = lo + bank_rr[0] % (hi - lo)
        bank_rr[0] += 1
        return b

    identf = ar.alloc([128], F32)
    identb = ar.alloc([128], BF16)
    g1 = ar.alloc([D], F32)
    gdl = ar.alloc([128], F32)
    gret = ar.alloc([512], F32)
    lamv = ar.alloc([256], F32)
    small = ar.alloc([64], F32)
    epst = small[:, 0:1]
    neglam = small[:, 1:2]
    lam_s1 = small[:, 2:3]
    lam_s2 = small[:, 3:4]
    junk = ar.alloc([D], F32)
    mix = ar.alloc([16, D], BF16)
    kdecp = ar.alloc([16], F32)
    rcoef = ar.alloc([NCOEF], F32)

    def ld(dst, src, key, R=(), W=()):
        sch.dma("sp", lambda e: e.dma_start(out=dst, in_=src), R=R, W=W, key=key)

    ld(identf, identf_d, "c0", W=["identf"])
    ld(identb, identb_d, "c1", W=["identb"])
    ld(g1, g1_d, "c2", W=["g1"])
    ld(gdl, gdiff_d, "c5", W=["gdl"])
    ld(gret, gret_d, "c6", W=["gret"])
    ld(lamv, lamv_d, "c7", W=["lamv"])
    ld(kdecp, kdecp_d, "c8", W=["kdecp"])
    ld(rcoef, rcoef_d, "c9", W=["rcoef"])

    sch.op("dve", lambda e: e.memset(small[:, 0:1], EPS), W=["small"])
    sch.op("dve", lambda e: e.scalar_tensor_tensor(out=junk[:, 0:64], in0=lamv[:, 0:64], scalar=1.0,
                                                   in1=lamv[:, 64:128], op0=ALU.mult, op1=ALU.mult,
                                                   accum_out=lam_s1), R=["lamv", "small"], W=["junk", "small"])
    sch.op("dve", lambda e: e.scalar_tensor_tensor(out=junk[:, 0:64], in0=lamv[:, 128:192], scalar=1.0,
                                                   in1=lamv[:, 192:256], op0=ALU.mult, op1=ALU.mult,
                                                   accum_out=lam_s2), R=["lamv", "small"], W=["junk", "small"])
    sch.op("act", lambda e: e.activation(out=small[:, 2:4], in_=small[:, 2:4], func=AF.Exp),
           R=["small"], W=["small"])
    sch.op("dve", lambda e: e.tensor_tensor(out=neglam, in0=lam_s2, in1=lam_s1, op=ALU.subtract),
           R=["small"], W=["small"])
    sch.op("dve", lambda e: e.tensor_scalar(out=neglam, in0=neglam, scalar1=-LAMBDA_INIT, scalar2=None,
                                            op0=ALU.add), R=["small"], W=["small"])
    sch.op("dve", lambda e: e.tensor_scalar(out=gdl, in0=gdl, scalar1=1.0 - LAMBDA_INIT, scalar2=None,
                                            op0=ALU.mult), R=["gdl"], W=["gdl"])

    persist_mark = ar.mark()

    def norm_stage(xt, xres, gt, gres, ht, hres, sstile, ssres):
        sch.op("dve", lambda e: e.scalar_tensor_tensor(out=junk, in0=xt, scalar=1.0, in1=xt, op0=ALU.mult,
                                                       op1=ALU.mult, accum_out=sstile),
               R=[xres], W=["junk", ssres])
        sch.op("act", lambda e: e.activation(out=sstile, in_=sstile, func=AF.Sqrt, bias=epst, scale=1.0 / D),
               R=[ssres, "small"], W=[ssres])
        sch.op("dve", lambda e: e.reciprocal(out=sstile, in_=sstile), R=[ssres], W=[ssres])
        sch.op("dve", lambda e: e.scalar_tensor_tensor(out=ht, in0=xt, scalar=sstile, in1=gt, op0=ALU.mult,
                                                       op1=ALU.mult), R=[xres, ssres, gres], W=[hres])

    def transpose_stage(ht, hres, hT, hTres, col0, eng):
        b = next_bank()
        pv = banks[b][:, :].bitcast(BF16)
        for k in range(8):
            sch.op("pe", lambda e, k=k: e.transpose(out=pv[:, k * 128:(k + 1) * 128],
                                                    in_=ht[:, k * 128:(k + 1) * 128], identity=identb),
                   R=[hres, "identb"], W=[("ps", b)])
        src = pv.rearrange("p (k t) -> p k t", k=8)
        dst = hT[:, :, col0:col0 + 128]
        if eng == "act":
            sch.op("act", lambda e: e.activation(out=dst, in_=src, func=AF.Copy), R=[("ps", b)], W=[hTres])
        else:
            sch.op("dve", lambda e: e.tensor_copy(out=dst, in_=src), R=[("ps", b)], W=[hTres])

    def proj_fm(win, c0, hT, hTres, b):
        for k in range(8):
            sch.op("pe", lambda e, k=k: e.matmul(banks[b][:, :], lhsT=win[:, k, c0:c0 + 128], rhs=hT[:, k, :],
                                                 start=(k == 0), stop=(k == 7)),
                   R=["win", hTres], W=[("ps", b)])

    def proj_tm(win, c0, hT, hTres, j, b):
        for k in range(8):
            sch.op("pe", lambda e, k=k: e.matmul(banks[b][:, :], lhsT=hT[:, k, j * 128:(j + 1) * 128],
                                                 rhs=win[:, k, c0:c0 + 512], start=(k == 0), stop=(k == 7)),
                   R=["win", hTres], W=[("ps", b)])

    class _Stop(Exception):
        pass

    try:
        win = ar.alloc([8, INW], BF16)
        for k in range(8):
            sch.dma("pool", lambda e, k=k: e.dma_start(out=win[:, k, :], in_=w_in[k * 128:(k + 1) * 128, :]),
                    W=["win"], key="win%d" % k)
        xb = [ar.alloc([D], F32) for _ in range(2)]
        hb = [ar.alloc([D], BF16) for _ in range(2)]
        hTb = [ar.alloc([8, 512], BF16) for _ in range(2)]
        ssb = [ar.alloc([1], F32) for _ in range(4)]
        _k = ar.alloc([8, 512], BF16)
        kst = [_k, _k]
        _v = ar.alloc([4, 4, VW], BF16)
        vst = [_v, _v]
        sch.op("pool", lambda e: e.memset(_v[:, :, :, 128:VW], 1.0), W=["vst"])
        Sown = ar.alloc([4, 512], F32)
        Sown_keep.append(Sown)
        go_mark = ar.mark()
        rkd = [ar.alloc([4, 512], BF16) for _ in range(2)]
        rvt = [ar.alloc([4, 512], BF16) for _ in range(2)]
        Sst = ar.alloc([512], F32)
        wstage = [ar.alloc([8, 512], BF16) for _ in range(2)]
        sch.op("dve", lambda e: e.memset(Sst, 0.0), W=["Sst"])
        sch.op("dve", lambda e: e.memset(Sown, 0.0), W=["Sown"])

        tile_ctr = [0]
        grp_ctr = [0]

        def run_pipeline(xsrc, ngroups, blocks_fn):
            ntiles = ngroups * 4
            base_t, base_g = tile_ctr[0], grp_ctr[0]

            def N(t):
                tt = base_t + t
                xs, ss = tt % 2, tt % 4
                sch.dma("sp", lambda e: e.dma_start(out=xb[xs], in_=xsrc[t * 128:(t + 1) * 128, :]),
                        W=[("xb", xs)], key="xb%d" % xs)
                norm_stage(xb[xs], ("xb", xs), g1, "g1", hb[xs], ("hb", xs), ssb[ss], ("ss", ss))

            def T(t):
                tt = base_t + t
                gp = (base_g + t // 4) % 2
                transpose_stage(hb[tt % 2], ("hb", tt % 2), hTb[gp], ("hT", gp), (t % 4) * 128,
                                "act" if tt % 2 == 0 else "dve")

            N(0)
            for t in range(4):
                if t + 1 < ntiles:
                    N(t + 1)
                T(t)
            for G in range(ngroups):
                gp = (base_g + G) % 2
                blocks = blocks_fn(G, hTb[gp], ("hT", gp))
                for q in range(4):
                    t = 4 * (G + 1) + q
                    if t + 1 < ntiles:
                        N(t + 1)
                    if t < ntiles:
                        T(t)
                    blocks[q]()
            tile_ctr[0] += ntiles
            grp_ctr[0] += ngroups

        def evac_kT(b, stage, stres, h, scale):
            sch.op("dve", lambda e: e.tensor_scalar(out=stage[0:64, 2 * h, :], in0=banks[b][0:64, :],
                                                    scalar1=scale, scalar2=None, op0=ALU.mult),
                   R=[("ps", b)], W=[stres])
            sch.op("act", lambda e: e.activation(out=stage[0:64, 2 * h + 1, :], in_=banks[b][64:128, :],
                                                 func=AF.Copy, scale=scale), R=[("ps", b)], W=[stres])

        def g_blocks(G, hT, hTres):
            p = G % 2

            def b0():
                for h in range(4):
                    b = next_bank()
                    proj_fm(win, 512 + h * 128, hT, hTres, b)
                    evac_kT(b, kst[p], "kst", h, 1.0)
                sch.dma("sp", lambda e: e.dma_start(
                    out=KTs[:, :, G * 512:(G + 1) * 512].rearrange("m p t -> p m t"), in_=kst[p][0:64, :, :]),
                    R=["kst"], W=["KTs"], key="kst")
                if G < 16:
                    ws = G % 2
                    if G < 8:
                        sch.dma("pool", lambda e: e.dma_start(
                            out=wstage[ws], in_=w_up[:, G * 512:(G + 1) * 512].rearrange("(k p) f -> p k f", p=128)),
                            W=[("wst", ws)], key="wstl%d" % ws)
                        sch.dma("pool", lambda e: e.dma_start(out=Wup_s[G], in_=wstage[ws]),
                                R=[("wst", ws)], W=["Wup_s"], key="wsts%d" % ws)
                    else:
                        n = G - 8
                        wv_ = wstage[ws].rearrange("p a b -> p (a b)").rearrange("p (c d) -> p c d", c=4)
                        sch.dma("pool", lambda e: e.dma_start(
                            out=wv_, in_=w_down[n * 512:(n + 1) * 512, :].rearrange("(c p) d -> p c d", p=128)),
                            W=[("wst", ws)], key="wstl%d" % ws)
                        sch.dma("pool", lambda e: e.dma_start(out=Wdn_s[n], in_=wv_),
                                R=[("wst", ws)], W=["Wdn_s"], key="wsts%d" % ws)

            def vrk(j):
                b = next_bank()
                proj_tm(win, 1024, hT, hTres, j, b)
                src = banks[b][:, :].rearrange("p (h v) -> p h v", h=4)
                if j % 2 == 0:
                    sch.op("act", lambda e: e.activation(out=vst[p][:, :, j, 0:128], in_=src, func=AF.Copy),
                           R=[("ps", b)], W=["vst"])
                else:
                    sch.op("dve", lambda e: e.tensor_copy(out=vst[p][:, :, j, 0:128], in_=src),
                           R=[("ps", b)], W=["vst"])
                b = next_bank()
                proj_tm(win, 2048, hT, hTres, j, b)
                for h in range(4):
                    if j % 2 == 1:
                        sch.op("act", lambda e, h=h, b=b: e.activation(
                            out=rkd[p][:, j, h * 128:(h + 1) * 128], in_=banks[b][:, h * 128:(h + 1) * 128],
                            func=AF.Copy, scale=kdecp[:, j * 4 + h:j * 4 + h + 1]),
                            R=[("ps", b), "kdecp"], W=[("rkd", p)])
                    else:
                        sch.op("dve", lambda e, h=h, b=b: e.tensor_scalar(
                            out=rkd[p][:, j, h * 128:(h + 1) * 128], in0=banks[b][:, h * 128:(h + 1) * 128],
                            scalar1=kdecp[:, j * 4 + h:j * 4 + h + 1], scalar2=None, op0=ALU.mult),
                            R=[("ps", b), "kdecp"], W=[("rkd", p)])
                b = next_bank()
                proj_tm(win, 2560, hT, hTres, j, b)
                if j % 2 == 0:
                    sch.op("dve", lambda e, b=b: e.tensor_copy(out=rvt[p][:, j, :], in_=banks[b][:, :]),
                           R=[("ps", b)], W=[("rvt", p)])
                else:
                    sch.op("act", lambda e, b=b: e.activation(out=rvt[p][:, j, :], in_=banks[b][:, :],
                                                              func=AF.Copy),
                           R=[("ps", b)], W=[("rvt", p)])

            def b1():
                vrk(0)
                vrk(1)

            def b2():
                vrk(2)
                vrk(3)
                sch.dma("sp", lambda e: e.dma_start(out=Vs[:, :, 4 * G:4 * G + 4, :], in_=vst[p]),
                        R=["vst"], W=["Vs"], key="vst")

            def b3():
                bw = next_bank()
                for h in range(4):
                    for j in range(4):
                        sch.op("pe", lambda e, h=h, j=j: e.matmul(
                            banks[bw][:, h * 128:(h + 1) * 128], lhsT=rkd[p][:, j, h * 128:(h + 1) * 128],
                            rhs=rvt[p][:, j, h * 128:(h + 1) * 128], start=(j == 0), stop=(j == 3),
                            skip_group_check=True),
                            R=[("rkd", p), ("rvt", p)], W=[("ps", bw)])
                i_slot = G // 8
                cp = G % 8
                for h in range(4):
                    hs = slice(h * 128, (h + 1) * 128)
                    if cp == 0:
                        sch.op("dve", lambda e, hs=hs, h=h: e.tensor_scalar(
                            out=Sown[:, i_slot, hs], in0=Sst[:, hs], scalar1=rcoef[:, 4 + h:5 + h], scalar2=None,
                            op0=ALU.mult), R=["Sst", "rcoef"], W=["Sown"])
                    sch.op("dve", lambda e, hs=hs, h=h: e.scalar_tensor_tensor(
                        out=Sown[:, i_slot, hs], in0=banks[bw][:, hs],
                        scalar=rcoef[:, 8 + cp * 4 + h:9 + cp * 4 + h],
                        in1=Sown[:, i_slot, hs], op0=ALU.mult, op1=ALU.add),
                        R=[("ps", bw), "rcoef", "Sown"], W=["Sown"])
                    sch.op("dve", lambda e, hs=hs, h=h: e.scalar_tensor_tensor(
                        out=Sst[:, hs], in0=Sst[:, hs], scalar=rcoef[:, h:h + 1], in1=banks[bw][:, hs],
                        op0=ALU.mult, op1=ALU.add), R=[("ps", bw), "rcoef", "Sst"], W=["Sst"])

            return [b0, b1, b2, b3]

        run_pipeline(xall, 32, g_blocks)

        if stop == "G":
            raise _Stop()
        sch.barrier()
        ar.reset(go_mark)
        rqT = ar.alloc([4, 512], BF16)
        rqdT = ar.alloc([4, 512], BF16)
        rkT = ar.alloc([4, 512], BF16)
        rvo = ar.alloc([4, 512], BF16)
        gate = ar.alloc([4, 512], F32)
        Sb = ar.alloc([4, 512], BF16)
        dm = ar.alloc([4, 512], F32)
        qdec = ar.alloc([4, 512], F32)
        PT = [ar.alloc([512], BF16) for _ in range(4)]
        osb = ar.alloc([512], F32)
        rss = ar.alloc([4], F32)
        ld(dm, dm_d, "c10", W=["dm"])
        ld(qdec, qdec_d, "c11", W=["qdec"])
        sch.op("act", lambda e: e.activation(out=Sb, in_=Sown, func=AF.Copy), R=["Sown"], W=["Sb"])

        def o_blocks(i, hT, hTres):
            p = i % 2

            def b0():
                for h in range(4):
                    b = next_bank()
                    proj_fm(win, h * 128, hT, hTres, b)
                    evac_kT(b, kst[0], "kst", h, 0.125)
                sch.dma("sp", lambda e: e.dma_start(
                    out=QTs[:, :, i * 512:(i + 1) * 512].rearrange("m p t -> p m t"), in_=kst[0][0:64, :, :]),
                    R=["kst"], W=["QTs"], key="kst")
                for h in range(4):
                    b = next_bank()
                    proj_fm(win, 512 + h * 128, hT, hTres, b)
                    evac_kT(b, kst[p], "kst", h, 1.0)
                sch.dma("sp", lambda e: e.dma_start(
                    out=KTo[:, :, i * 512:(i + 1) * 512].rearrange("m p t -> p m t"), in_=kst[p][0:64, :, :]),
                    R=["kst"], W=["KTo"], key="kst")

            def b1():
                for j in range(4):
                    b = next_bank()
                    proj_tm(win, 1024, hT, hTres, j, b)
                    src = banks[b][:, :].rearrange("p (h v) -> p h v", h=4)
                    sch.op("act", lambda e, src=src, j=j: e.activation(out=vst[p][:, :, j, 0:128], in_=src,
                                                                       func=AF.Copy),
                           R=[("ps", b)], W=["vst"])
                    b = next_bank()
                    proj_tm(win, 2560, hT, hTres, j, b)
                    sch.op("dve", lambda e, b=b, j=j: e.tensor_copy(out=rvo[:, j, :], in_=banks[b][:, :]),
                           R=[("ps", b)], W=["rvo"])
                    b = next_bank()
                    proj_tm(win, 3072, hT, hTres, j, b)
                    sch.op("act", lambda e, b=b, j=j: e.activation(out=gate[:, j, :], in_=banks[b][:, :],
                                                                   func=AF.Silu),
                           R=[("ps", b)], W=["gate"])
                    sch.op("pool", lambda e, j=j: e.tensor_tensor(out=gate[:, j, :], in0=gate[:, j, :], in1=gret,
                                                                  op=ALU.mult), R=["gate", "gret"], W=["gate"])
                sch.dma("sp", lambda e: e.dma_start(out=Vo[:, :, 4 * i:4 * i + 4, :], in_=vst[p]),
                        R=["vst"], W=["Vo"], key="vst")

            def b2():
                for h in range(4):
                    b = next_bank()
                    proj_fm(win, 1536 + h * 128, hT, hTres, b)
                    sch.op("act", lambda e, b=b, h=h: e.activation(out=rqT[:, h, :], in_=banks[b][:, :],
                                                                   func=AF.Copy),
                           R=[("ps", b)], W=["rqT"])
                    sch.op("dve", lambda e, b=b, h=h: e.tensor_tensor(out=rqdT[:, h, :], in0=banks[b][:, :],
                                                                      in1=qdec[:, h, :], op=ALU.mult),
                           R=[("ps", b), "qdec"], W=["rqdT"])
                    b = next_bank()
                    proj_fm(win, 2048 + h * 128, hT, hTres, b)
                    sch.op("act", lambda e, b=b, h=h: e.activation(out=rkT[:, h, :], in_=banks[b][:, :],
                                                                   func=AF.Copy, scale=128.0 ** -0.5),
                           R=[("ps", b)], W=["rkT"])

            def b3():
                for h in range(4):
                    hs = slice(h * 128, (h + 1) * 128)
                    for jk in range(4):
                        b = next_bank()
                        q0 = jk * 128
                        sch.op("pe", lambda e, b=b, jk=jk, q0=q0: e.matmul(
                            banks[b][:, q0:512], lhsT=rkT[:, h, jk * 128:(jk + 1) * 128], rhs=rqT[:, h, q0:512],
                            start=True, stop=True), R=["rkT", "rqT"], W=[("ps", b)])
                        sch.op("dve", lambda e, b=b, jk=jk, q0=q0: e.tensor_tensor(
                            out=PT[jk][:, q0:512], in0=banks[b][:, q0:512], in1=dm[:, h, 0:512 - q0], op=ALU.mult),
                            R=[("ps", b), "dm"], W=[("PT", jk)])
                    bo = next_bank()
                    for j in range(4):
                        js = slice(j * 128, (j + 1) * 128)
                        for jk in range(j + 1):
                            sch.op("pe", lambda e, js=js, jk=jk: e.matmul(
                                banks[bo][:, js], lhsT=PT[jk][:, js], rhs=rvo[:, jk, hs], start=(jk == 0),
                                stop=False, skip_group_check=True), R=[("PT", jk), "rvo"], W=[("ps", bo)])
                        sch.op("pe", lambda e, js=js: e.matmul(
                            banks[bo][:, js], lhsT=rqdT[:, h, js], rhs=Sb[:, i, hs], start=False, stop=True,
                            skip_group_check=True), R=["rqdT", "Sb"], W=[("ps", bo)])
                    sch.op("act", lambda e: e.activation(out=osb, in_=banks[bo][:, :], func=AF.Copy),
                           R=[("ps", bo)], W=["osb"])
                    for j in range(4):
                        js = slice(j * 128, (j + 1) * 128)
                        sch.op("dve", lambda e, js=js, j=j: e.scalar_tensor_tensor(
                            out=junk[:, 0:128], in0=osb[:, js], scalar=1.0, in1=osb[:, js], op0=ALU.mult,
                            op1=ALU.mult, accum_out=rss[:, j:j + 1]), R=["osb"], W=["junk", "rss"])
                    sch.op("act", lambda e: e.activation(out=rss, in_=rss, func=AF.Sqrt, bias=epst,
                                                         scale=1.0 / 128), R=["rss", "small"], W=["rss"])
                    sch.op("dve", lambda e: e.reciprocal(out=rss, in_=rss), R=["rss"], W=["rss"])
                    for j in range(4):
                        js = slice(j * 128, (j + 1) * 128)
                        sch.op("dve", lambda e, js=js, j=j: e.scalar_tensor_tensor(
                            out=mix[:, 4 * i + j, 512 + h * 128:512 + (h + 1) * 128], in0=osb[:, js],
                            scalar=rss[:, j:j + 1], in1=gate[:, j, hs], op0=ALU.mult, op1=ALU.mult),
                            R=["osb", "rss", "gate"], W=["mix"])

            return [b0, b1, b2, b3]

        run_pipeline(xown, NSLOT, o_blocks)

        if stop == "O":
            raise _Stop()
        sch.barrier()
        ar.reset(persist_mark)

        KT = [ar.alloc([S], BF16) for _ in range(2)]
        Vp = ar.alloc([128, VW], BF16)
        QT = [ar.alloc([OWN], BF16) for _ in range(2)]
        KTown = [ar.alloc([OWN], BF16) for _ in range(2)]
        Vop = ar.alloc([16, VW], BF16)
        cmask = ar.alloc([4, 512], BF16)
        Eb = [[ar.alloc([512], BF16) for _ in range(3)] for _ in range(2)]
        accs = ar.alloc([8, VW], F32)
        at = ar.alloc([128], F32)
        au = ar.alloc([128], F32)
        rc = ar.alloc([8], F32)
        dbgS = ar.alloc([512], F32)
        ld(cmask, cmask_d, "c12", W=["cmask"])
        for m in range(2):
            sch.dma("sp", lambda e, m=m: e.dma_start(out=KT[m][64:100, :], in_=kaug_d), W=[("KTaug", m)],
                    key="kaug%d" % m)
            sch.dma("sp", lambda e, m=m: e.dma_start(out=KTown[m][64:68, :], in_=kaugo_d), W=[("KToaug", m)],
                    key="kaugo%d" % m)

        SBANK = [[0, 1], [2, 3]]
        ABANK = [4, 5, 6]

        for h in range(4):
            for m in range(2):
                for rg in range(4):
                    cs = slice(rg * 4096, (rg + 1) * 4096)
                    sch.dma("sp", lambda e, m=m, cs=cs, h=h: e.dma_start(out=KT[m][0:64, cs],
                                                                         in_=KTs[2 * h + m, :, cs]),
                            R=["KTs"], W=[("KT", m, rg)], key="KT%d_%d" % (m, rg))
                sch.dma("sp", lambda e, m=m, h=h: e.dma_start(out=QT[m][0:64, :], in_=QTs[2 * h + m, :, :]),
                        R=["QTs"], W=[("QT", m)], key="QT%d" % m)
                sch.dma("sp", lambda e, m=m, h=h: e.dma_start(out=QT[m][64:100, :], in_=qaug_d[h, :, :]),
                        W=[("QTa", m)], key="QTa%d" % m)
                sch.dma("sp", lambda e, m=m, h=h: e.dma_start(out=KTown[m][0:64, :], in_=KTo[2 * h + m, :, :]),
                        R=["KTo"], W=[("KTown", m)], key="KTown%d" % m)
            for rg in range(4):
                ts_ = slice(rg * 32, (rg + 1) * 32)
                sch.dma("sp", lambda e, ts_=ts_, h=h: e.dma_start(out=Vp[:, ts_, :], in_=Vs[:, h, ts_, :]),
                        R=["Vs"], W=[("Vp", rg)], key="Vp%d" % rg)
            sch.dma("sp", lambda e, h=h: e.dma_start(out=Vop[:, :, :], in_=Vo[:, h, :, :]),
                    R=["Vo"], W=["Vop"], key="Vop")

            for i in range(NSLOT):
                qs = slice(i * 512, (i + 1) * 512)
                ntile = 32 * i + 32 + 4
                tiles = [("g", kb) for kb in range(32 * i + 32)] + [("o", jk) for jk in range(4)]

                def emit_qk(t):
                    kind, idx = tiles[t]
                    for m in range(2):
                        b = SBANK[m][t % 2]
                        if kind == "g":
                            rgn = idx // 32
                            R_ = 100 if idx >= 32 * i else 68
                            ks = slice(idx * 128, (idx + 1) * 128)
                            sch.op("pe", lambda e, b=b, m=m, R_=R_, ks=ks: e.matmul(
                                banks[b][:, :], lhsT=KT[m][0:R_, ks], rhs=QT[m][0:R_, qs], start=True, stop=True),
                                R=[("KT", m, rgn), ("KTaug", m), ("QT", m), ("QTa", m)], W=[("ps", b)])
                        else:
                            ks = slice((4 * i + idx) * 128, (4 * i + idx + 1) * 128)
                            sch.op("pe", lambda e, b=b, m=m, ks=ks: e.matmul(
                                banks[b][:, :], lhsT=KTown[m][0:68, ks], rhs=QT[m][0:68, qs], start=True, stop=False),
                                R=[("KTown", m), ("KToaug", m), ("QT", m), ("QTa", m)], W=[("ps", b)])
                            sch.op("pe", lambda e, b=b, idx=idx: e.matmul(
                                banks[b][:, :], lhsT=identb, rhs=cmask[:, idx, :], start=False, stop=True),
                                R=["identb", "cmask"], W=[("ps", b)])

                def emit_exp(t):
                    for m in range(2):
                        b = SBANK[m][t % 2]
                        eb = t % 3
                        sch.op("act", lambda e, b=b, m=m, eb=eb: e.activation(out=Eb[m][eb], in_=banks[b][:, :],
                                                                             func=AF.Exp),
                               R=[("ps", b)], W=[("E", m, eb)])

                def emit_av(t):
                    kind, idx = tiles[t]
                    eb = t % 3
                    for j in range(4):
                        for m in range(2):
                            a = j * 2 + m
                            b = ABANK[a // 3]
                            c0 = (a % 3) * VW
                            st = (t == 0 and a % 3 == 0)
                            if kind == "g":
                                rhs = Vp[:, idx, :]
                                vres = [("Vp", idx // 32)]
                            else:
                                rhs = Vop[:, 4 * i + idx, :]
                                vres = ["Vop"]
                            sch.op("pe", lambda e, b=b, c0=c0, m=m, eb=eb, j=j, rhs=rhs, st=st: e.matmul(
                                banks[b][:, c0:c0 + VW], lhsT=Eb[m][eb][:, j * 128:(j + 1) * 128], rhs=rhs,
                                start=st, stop=(t == ntile - 1), skip_group_check=True),
                                R=[("E", m, eb)] + vres, W=[("ps", b)])

                emit_qk(0)
                emit_qk(1)
                for t in range(ntile):
                    if debug and h == 0 and i == 0 and t == ntile - 1:
                        sch.op("dve", lambda e, t=t: e.tensor_copy(out=dbgS, in_=banks[SBANK[0][t % 2]][:, :]),
                               R=[("ps", SBANK[0][t % 2])], W=["dbgS"])
                        sch.dma("sp", lambda e: e.dma_start(out=dbg_s, in_=dbgS), R=["dbgS"], W=["dbgs_o"], key="dbgs")
                        sch.dma("sp", lambda e: e.dma_start(out=dbg_kq[:, 0:512], in_=KTown[0][:, 0:512]),
                                R=[("KTown", 0), ("KToaug", 0)], W=["dbgkq1"], key="dbgkq1")
                        sch.dma("sp", lambda e: e.dma_start(out=dbg_kq[:, 512:1024], in_=QT[0][:, 0:512]),
                                R=[("QT", 0), ("QTa", 0)], W=["dbgkq2"], key="dbgkq2")
                    emit_exp(t)
                    emit_av(t)
                    if t + 2 < ntile:
                        emit_qk(t + 2)

                for bi, b in enumerate(ABANK):
                    na = 3 if bi < 2 else 2
                    src = banks[b][:, 0:na * VW].rearrange("p (a c) -> p a c", a=na)
                    sch.op("dve", lambda e, src=src, bi=bi, na=na: e.tensor_copy(out=accs[:, bi * 3:bi * 3 + na, :],
                                                                                in_=src),
                           R=[("ps", b)], W=["accs"])
                if debug and h == 0 and i == 0:
                    sch.dma("sp", lambda e: e.dma_start(out=dbg_accs, in_=accs.rearrange("p a b -> p (a b)")),
                            R=["accs"], W=["dbgaccs"], key="dbgaccs")
                    sch.dma("sp", lambda e: e.dma_start(out=dbg_e, in_=Eb[0][(ntile - 1) % 3]),
                            R=[("E", 0, (ntile - 1) % 3)], W=["dbge"], key="dbge")
                sch.op("dve", lambda e: e.reciprocal(out=rc, in_=accs[:, :, 128]), R=["accs"], W=["rc"])
                for j in range(4):
                    sch.op("dve", lambda e, j=j: e.tensor_tensor(out=rc[:, 2 * j + 1:2 * j + 2],
                                                                 in0=rc[:, 2 * j + 1:2 * j + 2], in1=neglam,
                                                                 op=ALU.mult), R=["rc", "small"], W=["rc"])
                    sch.op("act", lambda e, j=j: e.activation(out=at, in_=accs[:, 2 * j, 0:128], func=AF.Copy,
                                                              scale=rc[:, 2 * j:2 * j + 1]),
                           R=["accs", "rc"], W=["at"])
                    sch.op("dve", lambda e, j=j: e.scalar_tensor_tensor(
                        out=au, in0=accs[:, 2 * j + 1, 0:128], scalar=rc[:, 2 * j + 1:2 * j + 2], in1=at,
                        op0=ALU.mult, op1=ALU.add), R=["accs", "rc", "at"], W=["au"])
                    sch.op("dve", lambda e, j=j: e.scalar_tensor_tensor(
                        out=junk[:, 0:128], in0=au, scalar=1.0, in1=au, op0=ALU.mult, op1=ALU.mult,
                        accum_out=small[:, 8 + j:9 + j]), R=["au"], W=["junk", ("dss", j)])
                    sch.op("act", lambda e, j=j: e.activation(out=small[:, 8 + j:9 + j], in_=small[:, 8 + j:9 + j],
                                                              func=AF.Sqrt, bias=epst, scale=1.0 / 128),
                           R=[("dss", j), "small"], W=[("dss", j)])
                    sch.op("dve", lambda e, j=j: e.reciprocal(out=small[:, 8 + j:9 + j], in_=small[:, 8 + j:9 + j]),
                           R=[("dss", j)], W=[("dss", j)])
                    sch.op("dve", lambda e, j=j, i=i, h=h: e.scalar_tensor_tensor(
                        out=mix[:, 4 * i + j, h * 128:(h + 1) * 128], in0=au, scalar=small[:, 8 + j:9 + j], in1=gdl,
                        op0=ALU.mult, op1=ALU.mult), R=["au", ("dss", j), "gdl"], W=["mix"])

        if stop == "A":
            raise _Stop()
        sch.barrier()
        ar.reset(persist_mark)

        g2 = ar.alloc([D], F32)
        gf = ar.alloc([D], F32)
        ld(g2, g2_d, "c3", W=["g2"])
        ld(gf, gf_d, "c4", W=["gf"])
        wout = ar.alloc([8, D], BF16)
        for k in range(8):
            sch.dma("pool", lambda e, k=k: e.dma_start(out=wout[:, k, :], in_=w_out[k * 128:(k + 1) * 128, :]),
                    W=["wout"], key="wout%d" % (k % 2))
        NWB = 4
        wbuf = [ar.alloc([8, 512], BF16) for _ in range(NWB)]
        x1 = ar.alloc([4, D], F32)
        h2 = [ar.alloc([D], BF16) for _ in range(2)]
        h2T = ar.alloc([8, 512], BF16)
        mixT = ar.alloc([8, 512], BF16)
        upT = ar.alloc([32, 512], BF16)
        rl = [ar.alloc([512], F32) for _ in range(2)]
        ob = [ar.alloc([D], F32) for _ in range(2)]
        ss2 = [ar.alloc([1], F32) for _ in range(4)]

        NCH = NSLOT * 16
        issued = [0]

        def issue_loads(upto):
            while issued[0] < min(upto, NCH):
                n = issued[0]
                issued[0] += 1
                s_ = n % NWB
                c = n % 16
                if c < 8:
                    sch.dma("sp", lambda e: e.dma_start(out=wbuf[s_], in_=Wup_s[c]), R=["Wup_s"],
                            W=[("wb", s_)], key="wb%d" % s_)
                else:
                    wv_ = wbuf[s_].rearrange("p a b -> p (a b)").rearrange("p (c d) -> p c d", c=4)
                    sch.dma("sp", lambda e: e.dma_start(out=wv_, in_=Wdn_s[c - 8]), R=["Wdn_s"],
                            W=[("wb", s_)], key="wb%d" % s_)

        issue_loads(3)
        mcount = [0]
        tcount = [0]
        for i in range(NSLOT):
            for j in range(4):
                tl = 4 * i + j
                transpose_stage(mix[:, tl, :], "mix", mixT, "mixT", j * 128, "act" if j % 2 == 0 else "dve")
            for j in range(4):
                tl = 4 * i + j
                t = tcount[0]
                tcount[0] += 1
                sch.dma("pool", lambda e: e.dma_start(out=x1[:, j, :], in_=xown[tl * 128:(tl + 1) * 128, :]),
                        W=[("x1", j)], key="x1_%d" % j)
                for half in range(2):
                    b = next_bank()
                    for k in range(8):
                        sch.op("pe", lambda e, k=k: e.matmul(
                            banks[b][:, :], lhsT=mixT[:, k, j * 128:(j + 1) * 128],
                            rhs=wout[:, k, half * 512:(half + 1) * 512], start=(k == 0), stop=(k == 7)),
                            R=["mixT", "wout"], W=[("ps", b)])
                    sch.op("dve", lambda e: e.tensor_tensor(
                        out=x1[:, j, half * 512:(half + 1) * 512], in0=banks[b][:, :],
                        in1=x1[:, j, half * 512:(half + 1) * 512], op=ALU.add), R=[("ps", b), ("x1", j)],
                        W=[("x1", j)])
                hs_ = t % 2
                norm_stage(x1[:, j, :], ("x1", j), g2, "g2", h2[hs_], ("h2", hs_), ss2[t % 4], ("ss2", t % 4))
                transpose_stage(h2[hs_], ("h2", hs_), h2T, "h2T", j * 128, "act" if j % 2 == 1 else "dve")
            for fc8 in range(8):
                n = i * 16 + fc8
                issue_loads(n + 4)
                s_ = n % NWB
                for f4 in range(4):
                    fc = fc8 * 4 + f4
                    b = next_bank()
                    for k in range(8):
                        sch.op("pe", lambda e, k=k: e.matmul(
                            banks[b][:, :], lhsT=wbuf[s_][:, k, f4 * 128:(f4 + 1) * 128], rhs=h2T[:, k, :],
                            start=(k == 0), stop=(k == 7)), R=[("wb", s_), "h2T"], W=[("ps", b)])
                    r = mcount[0] % 2
                    mcount[0] += 1
                    sch.op("act", lambda e: e.activation(out=rl[r], in_=banks[b][:, :], func=AF.Relu),
                           R=[("ps", b)], W=[("rl", r)])
                    sch.op("dve", lambda e: e.tensor_tensor(out=upT[:, fc, :], in0=rl[r], in1=rl[r], op=ALU.mult),
                           R=[("rl", r)], W=["upT"])
            dacc = [[next_bank() for _ in range(2)] for _ in range(4)]
            for fc8 in range(8):
                n = i * 16 + 8 + fc8
                issue_loads(n + 4)
                s_ = n % NWB
                wv = wbuf[s_].rearrange("p a b -> p (a b)").rearrange("p (c d) -> p c d", c=4)
                for j in range(4):
                    for half in range(2):
                        b = dacc[j][half]
                        for f4 in range(4):
                            fc = fc8 * 4 + f4
                            sch.op("pe", lambda e, f4=f4, fc=fc: e.matmul(
                                banks[b][:, :], lhsT=upT[:, fc, j * 128:(j + 1) * 128],
                                rhs=wv[:, f4, half * 512:(half + 1) * 512], start=(fc == 0), stop=(fc == 31),
                                skip_group_check=True), R=[("wb", s_), "upT"], W=[("ps", b)])
            for j in range(4):
                tl = 4 * i + j
                t = tcount[0]
                tcount[0] += 1
                for half in range(2):
                    b = dacc[j][half]
                    sch.op("dve", lambda e: e.tensor_tensor(
                        out=x1[:, j, half * 512:(half + 1) * 512], in0=banks[b][:, :],
                        in1=x1[:, j, half * 512:(half + 1) * 512], op=ALU.add), R=[("ps", b), ("x1", j)],
                        W=[("x1", j)])
                sst = ss2[t % 4]
                ssr = ("ss2", t % 4)
                o_ = t % 2
                sch.op("dve", lambda e: e.scalar_tensor_tensor(
                    out=junk, in0=x1[:, j, :], scalar=1.0, in1=x1[:, j, :], op0=ALU.mult, op1=ALU.mult,
                    accum_out=sst), R=[("x1", j)], W=["junk", ssr])
                sch.op("act", lambda e: e.activation(out=sst, in_=sst, func=AF.Sqrt, bias=epst, scale=1.0 / D),
                       R=[ssr, "small"], W=[ssr])
                sch.op("dve", lambda e: e.reciprocal(out=sst, in_=sst), R=[ssr], W=[ssr])
                sch.op("dve", lambda e: e.scalar_tensor_tensor(
                    out=ob[o_], in0=x1[:, j, :], scalar=sst, in1=gf, op0=ALU.mult, op1=ALU.mult),
                    R=[("x1", j), ssr, "gf"], W=[("ob", o_)])
                sch.dma("pool", lambda e: e.dma_start(out=out_d[tl * 128:(tl + 1) * 128, :], in_=ob[o_]),
                        R=[("ob", o_)], W=[("outd", tl)], key="ob%d" % o_)

        sch.barrier()

    except _Stop:
        pass
    sch.barrier()
    if debug:
        sch.dma("sp", lambda e: e.dma_start(out=dbg_mix.rearrange("(t p) d -> p t d", p=128), in_=mix),
                R=["mix"], W=["dbgmix"], key="dbgmix")
        sch.dma("sp", lambda e: e.dma_start(out=dbg_kt, in_=KTs[:, :, 0:2048]), R=["KTs"], W=["dbgkt"], key="dbgkt")
        sch.dma("sp", lambda e: e.dma_start(out=dbg_v, in_=Vs[:, :, 0:16, :]), R=["Vs"], W=["dbgv"], key="dbgv")
        sch.dma("sp", lambda e: e.dma_start(out=dbg_qt, in_=QTs), R=["QTs"], W=["dbgqt"], key="dbgqt")
        sch.dma("sp", lambda e: e.dma_start(out=dbg_sown, in_=Sown_keep[0].rearrange("p a b -> p (a b)")), R=["Sown"], W=["dbgsown"], key="dbgsown")
        sch.barrier()

    sem_ctx = []

    def semctx(name):
        cx = nc.semaphore(name)
        sem_ctx.append(cx)
        return cx.__enter__()

    sch.finalize(nc, semctx)
    with nc.Block() as block:
        @block.tensor
        def _(e):
            sch.run("pe", e)

        @block.scalar
        def _(e):
            sch.run("act", e)

        @block.vector
        def _(e):
            sch.run("dve", e)

        @block.gpsimd
        def _(e):
            sch.run("pool", e)

        @block.sync
        def _(e):
            sch.run("sp", e)

    for cx in reversed(sem_ctx):
        cx.__exit__(None, None, None)
    for cx in reversed(bank_ctx):
        cx.__exit__(None, None, None)
    ctx_arena.__exit__(None, None, None)
    return nc


def _tables():
    bf = ml_dtypes.bfloat16
    t = np.arange(S)
    kaug = np.zeros((36, S), np.float32)
    kaug[0] = 1.0
    kaug[1] = 1.0
    kaug[2] = t % 128
    kaug[3] = t // 128
    kaug[4 + ((t // 128) % 32), t] = 1.0
    cm = np.zeros((128, 4, 512), np.float32)
    ki = np.arange(128)[:, None]
    qi = np.arange(128)[None, :]
    for jk in range(4):
        for jq in range(4):
            blk = cm[:, jk, jq * 128:(jq + 1) * 128]
            if jq < jk:
                blk[:] = NEG
            elif jq == jk:
                blk[:] = np.where(ki > qi, NEG, 0.0)
    logg = np.log1p(-np.exp2(-5.0 - np.arange(4, dtype=np.float64)))
    dm = np.zeros((128, 4, 512), np.float64)
    qd = np.zeros((128, 4, 512), np.float64)
    kd = np.zeros((128, 16), np.float64)
    r = np.arange(512)
    for h in range(4):
        qd[:, h, :] = np.exp(logg[h] * (r + 1.0))[None, :]
        rel = r[None, :] - np.arange(128)[:, None]
        dm[:, h, :] = np.where(rel >= 0, np.exp(logg[h] * np.maximum(rel, 0)), 0.0)
        for j in range(4):
            kd[:, j * 4 + h] = np.exp(logg[h] * (511.0 - 128 * j - np.arange(128))) * (128.0 ** -0.5)
    return (kaug.astype(bf), cm.astype(bf), dm.astype(np.float32), qd.astype(np.float32),
            kd.astype(np.float32), logg)


def _core_tables(c, logg):
    bf = ml_dtypes.bfloat16
    n = np.arange(OWN)
    tpos = 512 * (8 * (n // 512) + c) + (n % 512)
    qaug = np.zeros((4, 36, OWN), np.float32)
    for h in range(4):
        sl = SLOPES[h]
        qaug[h, 0] = -sl * (tpos % 128)
        qaug[h, 1] = -sl * 128.0 * (tpos // 128)
        qaug[h, 2] = sl
        qaug[h, 3] = sl * 128.0
        for rr in range(32):
            qaug[h, 4 + rr] = NEG if rr >= 4 * c else 0.0
    kaugo = np.zeros((4, OWN), np.float32)
    kaugo[0] = 1.0
    kaugo[1] = 1.0
    kaugo[2] = tpos % 128
    kaugo[3] = tpos // 128
    rc = np.zeros((NCOEF,), np.float64)
    for h in range(4):
        rc[h] = np.exp(logg[h] * 512.0)
        rc[4 + h] = np.exp(logg[h] * 512.0 * c)
        for cp in range(8):
            rc[8 + cp * 4 + h] = np.exp(logg[h] * 512.0 * (c - 1 - cp)) if cp < c else 0.0
    rcoef = np.broadcast_to(rc.astype(np.float32)[None, :], (128, NCOEF)).copy()
    return qaug.astype(bf), kaugo.astype(bf), rcoef


_CACHE = {}


def kernel(x, norm1_g, w_in, lambda_q1, lambda_k1, lambda_q2, lambda_k2, diff_norm_g, ret_norm_g,
           w_out, norm2_g, w_up, w_down, final_norm_g, _debug=False, _stop=None):
    f32 = np.float32
    x2 = np.ascontiguousarray(np.asarray(x, f32).reshape(S, D))
    bc = lambda v, n: np.ascontiguousarray(np.broadcast_to(np.asarray(v, f32).reshape(1, n), (128, n)))
    lamv = np.concatenate([bc(lambda_q1, 64), bc(lambda_k1, 64), bc(lambda_q2, 64), bc(lambda_k2, 64)], axis=1)
    kaug, cm, dm, qd, kd, logg = _tables()
    common = {
        "xall": x2,
        "w_in": np.ascontiguousarray(np.asarray(w_in, f32).reshape(D, INW)),
        "w_out": np.ascontiguousarray(np.asarray(w_out, f32).reshape(D, D)),
        "w_up": np.ascontiguousarray(np.asarray(w_up, f32).reshape(D, DFF)),
        "w_down": np.ascontiguousarray(np.asarray(w_down, f32).reshape(DFF, D)),
        "g1": bc(norm1_g, D), "g2": bc(norm2_g, D), "gf": bc(final_norm_g, D),
        "gdiff": bc(diff_norm_g, 128), "gret": bc(ret_norm_g, 512), "lamv": np.ascontiguousarray(lamv),
        "kaug": kaug, "cmask": cm, "identf": np.eye(128, dtype=f32),
        "identb": np.eye(128, dtype=f32).astype(ml_dtypes.bfloat16),
        "dm": dm, "qdec": qd, "kdecp": kd,
    }
    in_maps = []
    x4 = x2.reshape(NSLOT, NCORES, 512, D)
    for c in range(NCORES):
        qaug, kaugo, rcoef = _core_tables(c, logg)
        m = dict(common)
        m["xown"] = np.ascontiguousarray(x4[:, c].reshape(OWN, D))
        m["qaug"] = qaug
        m["kaugo"] = kaugo
        m["rcoef"] = rcoef
        in_maps.append(m)
    key = (bool(_debug), _stop)
    if key not in _CACHE:
        _CACHE[key] = build_program(debug=bool(_debug), stop=_stop)
    nc = _CACHE[key]
    res = run_bass_kernel_spmd(nc, in_maps, core_ids=list(range(NCORES)))
    out = np.zeros((NSLOT, NCORES, 512, D), f32)
    for c in range(NCORES):
        out[:, c] = np.asarray(res.results[c]["out"], f32).reshape(NSLOT, 512, D)
    full = out.reshape(1, S, D)
    if _debug:
        mixd = np.zeros((NSLOT, NCORES, 512, D), f32)
        for c in range(NCORES):
            mixd[:, c] = np.asarray(res.results[c]["dbg_mix"]).astype(f32).reshape(NSLOT, 512, D)
        dbg = {"mix": mixd.reshape(S, D), "res": res.results}
        return full, dbg
    return full
```

```python
import math
import numpy as np
import ml_dtypes
import concourse.bass as bass
import concourse.mybir as mybir
from concourse.bass_utils import run_bass_kernel_spmd

F32 = mybir.dt.float32
BF16 = mybir.dt.bfloat16
AF = mybir.ActivationFunctionType
ALU = mybir.AluOpType

NCORES = 8
S = 16384
D = 1024
DFF = 4096
NSLOT = 4
OWN = 2048
INW = 3584
EPS = 1e-5
LAMBDA_INIT = 0.8 - 0.6 * math.exp(-0.3 * 0)
NEG = -30000.0
SLOPES = [2.0 ** (-2.0 * (h + 1)) for h in range(4)]
NCOEF = 40
WINDOWS = [4, 16, 64, None]
VW = 130


class _Rec:
    def __init__(self):
        self.call = None

    def __getattr__(self, name):
        def f(*a, **kw):
            assert self.call is None
            self.call = (name, a, kw)
            return self
        return f

    def replay(self, e):
        name, a, kw = self.call
        return getattr(e, name)(*a, **kw)


def _record_call(fn):
    r = _Rec()
    fn(r)
    assert r.call is not None
    return r.replay


class Sched:
    ENGS = ["pe", "act", "dve", "pool", "sp"]

    def __init__(self):
        self.ops = []
        self.lastw = {}
        self.rd_eng = {}
        self.rd_dma = {}
        self.last_of_eng = {}
        self.dmas_since_barrier = []

    def _deps(self, R, W):
        deps = set()
        for r in R:
            w = self.lastw.get(r)
            if w is not None:
                deps.add(w)
        for w_ in W:
            w = self.lastw.get(w_)
            if w is not None:
                deps.add(w)
            for v in self.rd_eng.get(w_, {}).values():
                deps.add(v)
            for v in self.rd_dma.get(w_, ()):
                deps.add(v)
        return deps

    def _record(self, idx, eng, R, W, is_dma):
        for r in R:
            if is_dma:
                self.rd_dma.setdefault(r, []).append(idx)
            else:
                self.rd_eng.setdefault(r, {})[eng] = idx
        for w_ in W:
            self.lastw[w_] = idx
            self.rd_eng[w_] = {}
            self.rd_dma[w_] = []
        self.last_of_eng[eng] = idx

    def op(self, eng, fn, R=(), W=()):
        idx = len(self.ops)
        if eng != "pe":
            W = list(W) + [("psr", r[1]) for r in R if isinstance(r, tuple) and r[0] == "ps"]
        deps = self._deps(R, W)
        self.ops.append(dict(eng=eng, fn=_record_call(fn), deps=deps, dma=False))
        self._record(idx, eng, R, W, False)
        return idx

    def dma(self, eng, fns, R=(), W=(), key=None):
        idx = len(self.ops)
        deps = self._deps(R, W)
        if not isinstance(fns, (list, tuple)):
            fns = [fns]
        self.ops.append(dict(eng=eng, fn=[_record_call(f) for f in fns], deps=deps, dma=True, key=key))
        self._record(idx, eng, R, W, True)
        self.dmas_since_barrier.append(idx)
        return idx

    def barrier(self):
        deps = set(self.last_of_eng.values()) | set(self.dmas_since_barrier)
        self.dmas_since_barrier = []
        for eng in self.ENGS:
            idx = len(self.ops)
            self.ops.append(dict(eng=eng, fn=None, deps=set(deps), dma=False))
            self.last_of_eng[eng] = idx

    def finalize(self, nc, semctx):
        needed = set()
        for i, o in enumerate(self.ops):
            for d in o["deps"]:
                od = self.ops[d]
                if od["eng"] == "pe" and o["eng"] == "pe" and not od["dma"] and not o["dma"]:
                    continue
                needed.add(d)
        self.eng_sem = {e: semctx("sem_" + e) for e in self.ENGS}
        self.key_sem = {}
        cnt = {e: 0 for e in self.ENGS}
        keycnt = {}
        for i, o in enumerate(self.ops):
            if o["dma"]:
                k = o["key"]
                if k not in self.key_sem:
                    self.key_sem[k] = semctx("dsem_%d" % len(self.key_sem))
                keycnt[k] = keycnt.get(k, 0) + len(o["fn"])
                o["sig"] = (self.key_sem[k], 16 * keycnt[k])
            elif o["fn"] is not None and i in needed:
                cnt[o["eng"]] += 1
                o["sig"] = (self.eng_sem[o["eng"]], cnt[o["eng"]])
                o["inc"] = True
            elif o["fn"] is None:
                o["sig"] = None
        self.nsem = len(self.key_sem) + len(self.ENGS)

    def run(self, engname, e):
        waited = {}
        for o in self.ops:
            if o["eng"] != engname:
                continue
            waits = {}
            for d in o["deps"]:
                od = self.ops[d]
                if od["eng"] == "pe" and engname == "pe" and not od["dma"] and not o["dma"]:
                    continue
                sig = od.get("sig")
                if sig is None:
                    continue
                sem, val = sig
                k = id(sem)
                if waited.get(k, (None, 0))[1] >= val:
                    continue
                if k not in waits or waits[k][1] < val:
                    waits[k] = (sem, val)
            for k, (sem, val) in waits.items():
                e.wait_ge(sem, val)
                waited[k] = (sem, val)
            if o["fn"] is None:
                continue
            if o["dma"]:
                sem, _ = o["sig"]
                for f in o["fn"]:
                    f(e).then_inc(sem, 16)
            else:
                ins = o["fn"](e)
                if o.get("inc"):
                    ins.then_inc(o["sig"][0], 1)


class Arena:
    def __init__(self, tensor, nbytes):
        self.t = tensor
        self.nbytes = nbytes
        self.off = 0

    def alloc(self, free_shape, dtype):
        esz = 4 if dtype == F32 else 2
        n = 1
        for s in free_shape:
            n *= s
        nb = n * esz
        self.off = (self.off + 63) // 64 * 64
        o = self.off
        self.off += nb
        assert self.off <= self.nbytes, ("SBUF arena overflow", self.off, self.nbytes)
        v = self.t[:, o // 2:(o + nb) // 2]
        if dtype == F32:
            v = v.bitcast(F32)
        if len(free_shape) == 2:
            v = v.rearrange("p (a b) -> p a b", a=free_shape[0])
        elif len(free_shape) == 3:
            v = v.rearrange("p (a b c) -> p a b c", a=free_shape[0], b=free_shape[1])
        return v

    def mark(self):
        return self.off

    def reset(self, m):
        self.off = m


def build_program(debug=False, stop=None):
    nc = bass.Bass("TRN2", target_bir_lowering=False)
    sch = Sched()

    def din(name, shape, dt=F32):
        return nc.dram_tensor(name, list(shape), dt, kind="ExternalInput").ap()

    xall = din("xall", [S, D])
    xown = din("xown", [OWN, D])
    w_in = din("w_in", [D, INW])
    w_out = din("w_out", [D, D])
    w_up = din("w_up", [D, DFF])
    w_down = din("w_down", [DFF, D])
    g1_d = din("g1", [128, D])
    g2_d = din("g2", [128, D])
    gf_d = din("gf", [128, D])
    gdiff_d = din("gdiff", [128, 128])
    gret_d = din("gret", [128, 512])
    lamv_d = din("lamv", [128, 256])
    kaug_d = din("kaug", [36, S], BF16)
    qaug_d = din("qaug", [4, 36, OWN], BF16)
    kaugo_d = din("kaugo", [4, OWN], BF16)
    cmask_d = din("cmask", [128, 4, 512], BF16)
    identf_d = din("identf", [128, 128])
    identb_d = din("identb", [128, 128], BF16)
    dm_d = din("dm", [128, 4, 512])
    qdec_d = din("qdec", [128, 4, 512])
    kdecp_d = din("kdecp", [128, 16])
    rcoef_d = din("rcoef", [128, NCOEF])
    out_d = nc.dram_tensor("out", [OWN, D], F32, kind="ExternalOutput").ap()
    if debug:
        dbg_mix = nc.dram_tensor("dbg_mix", [OWN, D], BF16, kind="ExternalOutput").ap()
        dbg_kt = nc.dram_tensor("dbg_kt", [8, 64, 2048], BF16, kind="ExternalOutput").ap()
        dbg_v = nc.dram_tensor("dbg_v", [128, 4, 16, VW], BF16, kind="ExternalOutput").ap()
        dbg_qt = nc.dram_tensor("dbg_qt", [8, 64, OWN], BF16, kind="ExternalOutput").ap()
        dbg_sown = nc.dram_tensor("dbg_sown", [128, 2048], F32, kind="ExternalOutput").ap()
        dbg_accs = nc.dram_tensor("dbg_accs", [128, 8 * VW], F32, kind="ExternalOutput").ap()
        dbg_e = nc.dram_tensor("dbg_e", [128, 512], BF16, kind="ExternalOutput").ap()
        dbg_s = nc.dram_tensor("dbg_s", [128, 512], F32, kind="ExternalOutput").ap()
        dbg_kq = nc.dram_tensor("dbg_kq", [128, 1024], BF16, kind="ExternalOutput").ap()
    Sown_keep = []

    KTs = nc.dram_tensor("KTs", [8, 64, S], BF16).ap()
    Vs = nc.dram_tensor("Vs", [128, 4, 128, VW], BF16).ap()
    QTs = nc.dram_tensor("QTs", [8, 64, OWN], BF16).ap()
    KTo = nc.dram_tensor("KTo", [8, 64, OWN], BF16).ap()
    Vo = nc.dram_tensor("Vo", [128, 4, 16, VW], BF16).ap()
    Wup_s = nc.dram_tensor("Wup_s", [8, 128, 8, 512], BF16).ap()
    Wdn_s = nc.dram_tensor("Wdn_s", [8, 128, 4, 1024], BF16).ap()

    ARENA_BYTES = 206 * 1024
    ctx_arena = nc.sbuf_tensor("arena", [128, ARENA_BYTES // 2], BF16)
    arena_t = ctx_arena.__enter__()
    ar = Arena(arena_t, ARENA_BYTES)
    banks = []
    bank_ctx = []
    for b in range(8):
        cx = nc.psum_tensor("psb%d" % b, [128, 512], F32)
        bank_ctx.append(cx)
        banks.append(cx.__enter__())

    bank_rr = [0]

    def next_bank(lo=0, hi=8):
        b = lo + bank_rr[0] % (hi - lo)
        bank_rr[0] += 1
        return b

    identf = ar.alloc([128], F32)
    identb = ar.alloc([128], BF16)
    g1 = ar.alloc([D], F32)
    gdl = ar.alloc([128], F32)
    gret = ar.alloc([512], F32)
    lamv = ar.alloc([256], F32)
    small = ar.alloc([64], F32)
    epst = small[:, 0:1]
    neglam = small[:, 1:2]
    lam_s1 = small[:, 2:3]
    lam_s2 = small[:, 3:4]
    junk = ar.alloc([D], F32)
    mix = ar.alloc([16, D], BF16)
    kdecp = ar.alloc([16], F32)
    rcoef = ar.alloc([NCOEF], F32)

    def ld(dst, src, key, R=(), W=()):
        sch.dma("sp", lambda e: e.dma_start(out=dst, in_=src), R=R, W=W, key=key)

    ld(identf, identf_d, "c0", W=["identf"])
    ld(identb, identb_d, "c1", W=["identb"])
    ld(g1, g1_d, "c2", W=["g1"])
    ld(gdl, gdiff_d, "c5", W=["gdl"])
    ld(gret, gret_d, "c6", W=["gret"])
    ld(lamv, lamv_d, "c7", W=["lamv"])
    ld(kdecp, kdecp_d, "c8", W=["kdecp"])
    ld(rcoef, rcoef_d, "c9", W=["rcoef"])

    sch.op("dve", lambda e: e.memset(small[:, 0:1], EPS), W=["small"])
    sch.op("dve", lambda e: e.scalar_tensor_tensor(out=junk[:, 0:64], in0=lamv[:, 0:64], scalar=1.0,
                                                   in1=lamv[:, 64:128], op0=ALU.mult, op1=ALU.mult,
                                                   accum_out=lam_s1), R=["lamv", "small"], W=["junk", "small"])
    sch.op("dve", lambda e: e.scalar_tensor_tensor(out=junk[:, 0:64], in0=lamv[:, 128:192], scalar=1.0,
                                                   in1=lamv[:, 192:256], op0=ALU.mult, op1=ALU.mult,
                                                   accum_out=lam_s2), R=["lamv", "small"], W=["junk", "small"])
    sch.op("act", lambda e: e.activation(out=small[:, 2:4], in_=small[:, 2:4], func=AF.Exp),
           R=["small"], W=["small"])
    sch.op("dve", lambda e: e.tensor_tensor(out=neglam, in0=lam_s2, in1=lam_s1, op=ALU.subtract),
           R=["small"], W=["small"])
    sch.op("dve", lambda e: e.tensor_scalar(out=neglam, in0=neglam, scalar1=-LAMBDA_INIT, scalar2=None,
                                            op0=ALU.add), R=["small"], W=["small"])
    sch.op("dve", lambda e: e.tensor_scalar(out=gdl, in0=gdl, scalar1=1.0 - LAMBDA_INIT, scalar2=None,
                                            op0=ALU.mult), R=["gdl"], W=["gdl"])

    persist_mark = ar.mark()

    def norm_stage(xt, xres, gt, gres, ht, hres, sstile, ssres):
        sch.op("dve", lambda e: e.scalar_tensor_tensor(out=junk, in0=xt, scalar=1.0, in1=xt, op0=ALU.mult,
                                                       op1=ALU.mult, accum_out=sstile),
               R=[xres], W=["junk", ssres])
        sch.op("act", lambda e: e.activation(out=sstile, in_=sstile, func=AF.Sqrt, bias=epst, scale=1.0 / D),
               R=[ssres, "small"], W=[ssres])
        sch.op("dve", lambda e: e.reciprocal(out=sstile, in_=sstile), R=[ssres], W=[ssres])
        sch.op("dve", lambda e: e.scalar_tensor_tensor(out=ht, in0=xt, scalar=sstile, in1=gt, op0=ALU.mult,
                                                       op1=ALU.mult), R=[xres, ssres, gres], W=[hres])

    def transpose_stage(ht, hres, hT, hTres, col0, eng):
        b = next_bank()
        pv = banks[b][:, :].bitcast(BF16)
        for k in range(8):
            sch.op("pe", lambda e, k=k: e.transpose(out=pv[:, k * 128:(k + 1) * 128],
                                                    in_=ht[:, k * 128:(k + 1) * 128], identity=identb),
                   R=[hres, "identb"], W=[("ps", b)])
        src = pv.rearrange("p (k t) -> p k t", k=8)
        dst = hT[:, :, col0:col0 + 128]
        if eng == "act":
            sch.op("act", lambda e: e.activation(out=dst, in_=src, func=AF.Copy), R=[("ps", b)], W=[hTres])
        else:
            sch.op("dve", lambda e: e.tensor_copy(out=dst, in_=src), R=[("ps", b)], W=[hTres])

    def proj_fm(win, c0, hT, hTres, b):
        for k in range(8):
            sch.op("pe", lambda e, k=k: e.matmul(banks[b][:, :], lhsT=win[:, k, c0:c0 + 128], rhs=hT[:, k, :],
                                                 start=(k == 0), stop=(k == 7)),
                   R=[("win", c0 // 512), hTres], W=[("ps", b)])

    def proj_tm(win, c0, hT, hTres, j, b):
        for k in range(8):
            sch.op("pe", lambda e, k=k: e.matmul(banks[b][:, :], lhsT=hT[:, k, j * 128:(j + 1) * 128],
                                                 rhs=win[:, k, c0:c0 + 512], start=(k == 0), stop=(k == 7)),
                   R=[("win", c0 // 512), hTres], W=[("ps", b)])

    class _Stop(Exception):
        pass

    try:
        win = ar.alloc([8, INW], BF16)
        for cb in (1, 2, 4, 5, 0, 3, 6):
            sch.dma("pool", lambda e, cb=cb: e.dma_start(
                out=win[:, :, cb * 512:(cb + 1) * 512],
                in_=w_in[:, cb * 512:(cb + 1) * 512].rearrange("(k p) c -> p k c", p=128)),
                W=[("win", cb)], key="win%d" % cb)
        xb = [ar.alloc([D], F32) for _ in range(2)]
        hb = [ar.alloc([D], BF16) for _ in range(2)]
        hTb = [ar.alloc([8, 512], BF16) for _ in range(2)]
        ssb = [ar.alloc([1], F32) for _ in range(4)]
        _k = ar.alloc([8, 512], BF16)
        kst = [_k, _k]
        _v = ar.alloc([4, 4, VW], BF16)
        vst = [_v, _v]
        sch.op("pool", lambda e: e.memset(_v[:, :, :, 128:VW], 1.0), W=["vst"])
        Sown = ar.alloc([4, 512], F32)
        Sown_keep.append(Sown)
        go_mark = ar.mark()
        rkd = [ar.alloc([4, 512], BF16) for _ in range(2)]
        rvt = [ar.alloc([4, 512], BF16) for _ in range(2)]
        Sst = ar.alloc([512], F32)
        wstage = [ar.alloc([8, 512], BF16) for _ in range(2)]
        sch.op("dve", lambda e: e.memset(Sst, 0.0), W=["Sst"])
        sch.op("dve", lambda e: e.memset(Sown, 0.0), W=["Sown"])

        tile_ctr = [0]
        grp_ctr = [0]

        def run_pipeline(xsrc, ngroups, blocks_fn):
            ntiles = ngroups * 4
            base_t, base_g = tile_ctr[0], grp_ctr[0]

            def N(t):
                tt = base_t + t
                xs, ss = tt % 2, tt % 4
                sch.dma("sp", lambda e: e.dma_start(out=xb[xs], in_=xsrc[t * 128:(t + 1) * 128, :]),
                        W=[("xb", xs)], key="xb%d" % xs)
                norm_stage(xb[xs], ("xb", xs), g1, "g1", hb[xs], ("hb", xs), ssb[ss], ("ss", ss))

            def T(t):
                tt = base_t + t
                gp = (base_g + t // 4) % 2
                transpose_stage(hb[tt % 2], ("hb", tt % 2), hTb[gp], ("hT", gp), (t % 4) * 128,
                                "act" if tt % 2 == 0 else "dve")

            N(0)
            for t in range(4):
                if t + 1 < ntiles:
                    N(t + 1)
                T(t)
            for G in range(ngroups):
                gp = (base_g + G) % 2
                blocks = blocks_fn(G, hTb[gp], ("hT", gp))
                for q in range(4):
                    t = 4 * (G + 1) + q
                    if t + 1 < ntiles:
                        N(t + 1)
                    if t < ntiles:
                        T(t)
                    blocks[q]()
            tile_ctr[0] += ntiles
            grp_ctr[0] += ngroups

        def evac_kT(b, stage, stres, h, scale):
            sch.op("dve", lambda e: e.tensor_scalar(out=stage[0:64, 2 * h, :], in0=banks[b][0:64, :],
                                                    scalar1=scale, scalar2=None, op0=ALU.mult),
                   R=[("ps", b)], W=[stres])
            sch.op("act", lambda e: e.activation(out=stage[0:64, 2 * h + 1, :], in_=banks[b][64:128, :],
                                                 func=AF.Copy, scale=scale), R=[("ps", b)], W=[stres])

        def g_blocks(G, hT, hTres):
            p = G % 2

            def b0():
                for h in range(4):
                    b = next_bank()
                    proj_fm(win, 512 + h * 128, hT, hTres, b)
                    evac_kT(b, kst[p], "kst", h, 1.0)
                sch.dma("sp", lambda e: e.dma_start(
                    out=KTs[:, :, G * 512:(G + 1) * 512].rearrange("m p t -> p m t"), in_=kst[p][0:64, :, :]),
                    R=["kst"], W=["KTs"], key="kst")
                if G < 16:
                    ws = G % 2
                    if G < 8:
                        sch.dma("pool", lambda e: e.dma_start(
                            out=wstage[ws], in_=w_up[:, G * 512:(G + 1) * 512].rearrange("(k p) f -> p k f", p=128)),
                            W=[("wst", ws)], key="wstl%d" % ws)
                        sch.dma("pool", lambda e: e.dma_start(out=Wup_s[G], in_=wstage[ws]),
                                R=[("wst", ws)], W=["Wup_s"], key="wsts%d" % ws)
                    else:
                        n = G - 8
                        wv_ = wstage[ws].rearrange("p a b -> p (a b)").rearrange("p (c d) -> p c d", c=4)
                        sch.dma("pool", lambda e: e.dma_start(
                            out=wv_, in_=w_down[n * 512:(n + 1) * 512, :].rearrange("(c p) d -> p c d", p=128)),
                            W=[("wst", ws)], key="wstl%d" % ws)
                        sch.dma("pool", lambda e: e.dma_start(out=Wdn_s[n], in_=wv_),
                                R=[("wst", ws)], W=["Wdn_s"], key="wsts%d" % ws)

            def vrk(j):
                b = next_bank()
                proj_tm(win, 1024, hT, hTres, j, b)
                src = banks[b][:, :].rearrange("p (h v) -> p h v", h=4)
                if j % 2 == 0:
                    sch.op("act", lambda e: e.activation(out=vst[p][:, :, j, 0:128], in_=src, func=AF.Copy),
                           R=[("ps", b)], W=["vst"])
                else:
                    sch.op("dve", lambda e: e.tensor_copy(out=vst[p][:, :, j, 0:128], in_=src),
                           R=[("ps", b)], W=["vst"])
                b = next_bank()
                proj_tm(win, 2048, hT, hTres, j, b)
                for h in range(4):
                    if j % 2 == 1:
                        sch.op("act", lambda e, h=h, b=b: e.activation(
                            out=rkd[p][:, j, h * 128:(h + 1) * 128], in_=banks[b][:, h * 128:(h + 1) * 128],
                            func=AF.Copy, scale=kdecp[:, j * 4 + h:j * 4 + h + 1]),
                            R=[("ps", b), "kdecp"], W=[("rkd", p)])
                    else:
                        sch.op("dve", lambda e, h=h, b=b: e.tensor_scalar(
                            out=rkd[p][:, j, h * 128:(h + 1) * 128], in0=banks[b][:, h * 128:(h + 1) * 128],
                            scalar1=kdecp[:, j * 4 + h:j * 4 + h + 1], scalar2=None, op0=ALU.mult),
                            R=[("ps", b), "kdecp"], W=[("rkd", p)])
                b = next_bank()
                proj_tm(win, 2560, hT, hTres, j, b)
                if j % 2 == 0:
                    sch.op("dve", lambda e, b=b: e.tensor_copy(out=rvt[p][:, j, :], in_=banks[b][:, :]),
                           R=[("ps", b)], W=[("rvt", p)])
                else:
                    sch.op("act", lambda e, b=b: e.activation(out=rvt[p][:, j, :], in_=banks[b][:, :],
                                                              func=AF.Copy),
                           R=[("ps", b)], W=[("rvt", p)])

            def b1():
                vrk(0)
                vrk(1)

            def b2():
                vrk(2)
                vrk(3)
                sch.dma("sp", lambda e: e.dma_start(out=Vs[:, :, 4 * G:4 * G + 4, :], in_=vst[p]),
                        R=["vst"], W=["Vs"], key="vst")

            def b3():
                bw = next_bank()
                for h in range(4):
                    for j in range(4):
                        sch.op("pe", lambda e, h=h, j=j: e.matmul(
                            banks[bw][:, h * 128:(h + 1) * 128], lhsT=rkd[p][:, j, h * 128:(h + 1) * 128],
                            rhs=rvt[p][:, j, h * 128:(h + 1) * 128], start=(j == 0), stop=(j == 3),
                            skip_group_check=True),
                            R=[("rkd", p), ("rvt", p)], W=[("ps", bw)])
                i_slot = G // 8
                cp = G % 8
                for h in range(4):
                    hs = slice(h * 128, (h + 1) * 128)
                    if cp == 0:
                        sch.op("dve", lambda e, hs=hs, h=h: e.tensor_scalar(
                            out=Sown[:, i_slot, hs], in0=Sst[:, hs], scalar1=rcoef[:, 4 + h:5 + h], scalar2=None,
                            op0=ALU.mult), R=["Sst", "rcoef"], W=["Sown"])
                    sch.op("dve", lambda e, hs=hs, h=h: e.scalar_tensor_tensor(
                        out=Sown[:, i_slot, hs], in0=banks[bw][:, hs],
                        scalar=rcoef[:, 8 + cp * 4 + h:9 + cp * 4 + h],
                        in1=Sown[:, i_slot, hs], op0=ALU.mult, op1=ALU.add),
                        R=[("ps", bw), "rcoef", "Sown"], W=["Sown"])
                    sch.op("dve", lambda e, hs=hs, h=h: e.scalar_tensor_tensor(
                        out=Sst[:, hs], in0=Sst[:, hs], scalar=rcoef[:, h:h + 1], in1=banks[bw][:, hs],
                        op0=ALU.mult, op1=ALU.add), R=[("ps", bw), "rcoef", "Sst"], W=["Sst"])

            return [b0, b1, b2, b3]

        run_pipeline(xall, 32, g_blocks)

        if stop == "G":
            raise _Stop()
        sch.barrier()
        ar.reset(go_mark)
        rqT = ar.alloc([4, 512], BF16)
        rqdT = ar.alloc([4, 512], BF16)
        rkT = ar.alloc([4, 512], BF16)
        rvo = ar.alloc([4, 512], BF16)
        gate = ar.alloc([4, 512], F32)
        Sb = ar.alloc([4, 512], BF16)
        dm = ar.alloc([4, 512], F32)
        qdec = ar.alloc([4, 512], F32)
        PT = [ar.alloc([512], BF16) for _ in range(4)]
        osb = ar.alloc([512], F32)
        rss = ar.alloc([4], F32)
        ld(dm, dm_d, "c10", W=["dm"])
        ld(qdec, qdec_d, "c11", W=["qdec"])
        sch.op("act", lambda e: e.activation(out=Sb, in_=Sown, func=AF.Copy), R=["Sown"], W=["Sb"])

        def o_blocks(i, hT, hTres):
            p = i % 2

            def b0():
                for h in range(4):
                    b = next_bank()
                    proj_fm(win, h * 128, hT, hTres, b)
                    evac_kT(b, kst[0], "kst", h, 0.125)
                sch.dma("sp", lambda e: e.dma_start(
                    out=QTs[:, :, i * 512:(i + 1) * 512].rearrange("m p t -> p m t"), in_=kst[0][0:64, :, :]),
                    R=["kst"], W=["QTs"], key="kst")
                for h in range(4):
                    b = next_bank()
                    proj_fm(win, 512 + h * 128, hT, hTres, b)
                    evac_kT(b, kst[p], "kst", h, 1.0)
                sch.dma("sp", lambda e: e.dma_start(
                    out=KTo[:, :, i * 512:(i + 1) * 512].rearrange("m p t -> p m t"), in_=kst[p][0:64, :, :]),
                    R=["kst"], W=["KTo"], key="kst")

            def b1():
                for j in range(4):
                    b = next_bank()
                    proj_tm(win, 1024, hT, hTres, j, b)
                    src = banks[b][:, :].rearrange("p (h v) -> p h v", h=4)
                    sch.op("act", lambda e, src=src, j=j: e.activation(out=vst[p][:, :, j, 0:128], in_=src,
                                                                       func=AF.Copy),
                           R=[("ps", b)], W=["vst"])
                    b = next_bank()
                    proj_tm(win, 2560, hT, hTres, j, b)
                    sch.op("dve", lambda e, b=b, j=j: e.tensor_copy(out=rvo[:, j, :], in_=banks[b][:, :]),
                           R=[("ps", b)], W=["rvo"])
                    b = next_bank()
                    proj_tm(win, 3072, hT, hTres, j, b)
                    sch.op("act", lambda e, b=b, j=j: e.activation(out=gate[:, j, :], in_=banks[b][:, :],
                                                                   func=AF.Silu),
                           R=[("ps", b)], W=["gate"])
                    sch.op("pool", lambda e, j=j: e.tensor_tensor(out=gate[:, j, :], in0=gate[:, j, :], in1=gret,
                                                                  op=ALU.mult), R=["gate", "gret"], W=["gate"])
                sch.dma("sp", lambda e: e.dma_start(out=Vo[:, :, 4 * i:4 * i + 4, :], in_=vst[p]),
                        R=["vst"], W=["Vo"], key="vst")

            def b2():
                for h in range(4):
                    b = next_bank()
                    proj_fm(win, 1536 + h * 128, hT, hTres, b)
                    sch.op("act", lambda e, b=b, h=h: e.activation(out=rqT[:, h, :], in_=banks[b][:, :],
                                                                   func=AF.Copy),
                           R=[("ps", b)], W=["rqT"])
                    sch.op("dve", lambda e, b=b, h=h: e.tensor_tensor(out=rqdT[:, h, :], in0=banks[b][:, :],
                                                                      in1=qdec[:, h, :], op=ALU.mult),
                           R=[("ps", b), "qdec"], W=["rqdT"])
                    b = next_bank()
                    proj_fm(win, 2048 + h * 128, hT, hTres, b)
                    sch.op("act", lambda e, b=b, h=h: e.activation(out=rkT[:, h, :], in_=banks[b][:, :],
                                                                   func=AF.Copy, scale=128.0 ** -0.5),
                           R=[("ps", b)], W=["rkT"])

            def b3():
                for h in range(4):
                    hs = slice(h * 128, (h + 1) * 128)
                    for jk in range(4):
                        b = next_bank()
                        q0 = jk * 128
                        sch.op("pe", lambda e, b=b, jk=jk, q0=q0: e.matmul(
                            banks[b][:, q0:512], lhsT=rkT[:, h, jk * 128:(jk + 1) * 128], rhs=rqT[:, h, q0:512],
                            start=True, stop=True), R=["rkT", "rqT"], W=[("ps", b)])
                        sch.op("dve", lambda e, b=b, jk=jk, q0=q0: e.tensor_tensor(
                            out=PT[jk][:, q0:512], in0=banks[b][:, q0:512], in1=dm[:, h, 0:512 - q0], op=ALU.mult),
                            R=[("ps", b), "dm"], W=[("PT", jk)])
                    bo = next_bank()
                    for j in range(4):
                        js = slice(j * 128, (j + 1) * 128)
                        for jk in range(j + 1):
                            sch.op("pe", lambda e, js=js, jk=jk: e.matmul(
                                banks[bo][:, js], lhsT=PT[jk][:, js], rhs=rvo[:, jk, hs], start=(jk == 0),
                                stop=False, skip_group_check=True), R=[("PT", jk), "rvo"], W=[("ps", bo)])
                        sch.op("pe", lambda e, js=js: e.matmul(
                            banks[bo][:, js], lhsT=rqdT[:, h, js], rhs=Sb[:, i, hs], start=False, stop=True,
                            skip_group_check=True), R=["rqdT", "Sb"], W=[("ps", bo)])
                    sch.op("act", lambda e: e.activation(out=osb, in_=banks[bo][:, :], func=AF.Copy),
                           R=[("ps", bo)], W=["osb"])
                    for j in range(4):
                        js = slice(j * 128, (j + 1) * 128)
                        sch.op("dve", lambda e, js=js, j=j: e.scalar_tensor_tensor(
                            out=junk[:, 0:128], in0=osb[:, js], scalar=1.0, in1=osb[:, js], op0=ALU.mult,
                            op1=ALU.mult, accum_out=rss[:, j:j + 1]), R=["osb"], W=["junk", "rss"])
                    sch.op("act", lambda e: e.activation(out=rss, in_=rss, func=AF.Sqrt, bias=epst,
                                                         scale=1.0 / 128), R=["rss", "small"], W=["rss"])
                    sch.op("dve", lambda e: e.reciprocal(out=rss, in_=rss), R=["rss"], W=["rss"])
                    for j in range(4):
                        js = slice(j * 128, (j + 1) * 128)
                        sch.op("dve", lambda e, js=js, j=j: e.scalar_tensor_tensor(
                            out=mix[:, 4 * i + j, 512 + h * 128:512 + (h + 1) * 128], in0=osb[:, js],
                            scalar=rss[:, j:j + 1], in1=gate[:, j, hs], op0=ALU.mult, op1=ALU.mult),
                            R=["osb", "rss", "gate"], W=["mix"])

            return [b0, b1, b2, b3]

        run_pipeline(xown, NSLOT, o_blocks)

        if stop == "O":
            raise _Stop()
        sch.barrier()
        ar.reset(persist_mark)

        KT = [ar.alloc([S], BF16) for _ in range(2)]
        Vp = ar.alloc([128, VW], BF16)
        QT2 = [[ar.alloc([OWN], BF16) for _ in range(2)] for _ in range(2)]
        KTown2 = [[ar.alloc([OWN], BF16) for _ in range(2)] for _ in range(2)]
        Vop2 = [ar.alloc([16, VW], BF16) for _ in range(2)]
        cmask = ar.alloc([4, 512], BF16)
        Eb = [[ar.alloc([512], BF16) for _ in range(3)] for _ in range(2)]
        accs = ar.alloc([8, VW], F32)
        at = ar.alloc([128], F32)
        au = ar.alloc([128], F32)
        rc = ar.alloc([8], F32)
        dbgS = ar.alloc([512], F32)
        ld(cmask, cmask_d, "c12", W=["cmask"])
        for m in range(2):
            sch.dma("sp", lambda e, m=m: e.dma_start(out=KT[m][64:100, :], in_=kaug_d), W=[("KTaug", m)],
                    key="kaug%d" % m)
            for hp_ in range(2):
                sch.dma("sp", lambda e, m=m, hp_=hp_: e.dma_start(out=KTown2[hp_][m][64:68, :], in_=kaugo_d),
                        W=[("KToaug", hp_, m)], key="kaugo%d_%d" % (hp_, m))

        SBANK = [[0, 1], [2, 3]]
        ABANK = [4, 5, 6]

        for h in range(4):
            hp = h % 2
            QT, KTown, Vop = QT2[hp], KTown2[hp], Vop2[hp]
            W_h = WINDOWS[h]
            for m in range(2):
                sch.dma("sp", lambda e, m=m: e.dma_start(out=QT[m][0:64, :], in_=QTs[2 * h + m, :, :]),
                        R=["QTs"], W=[("QT", hp, m)], key="QT%d_%d" % (hp, m))
                sch.dma("sp", lambda e, m=m: e.dma_start(out=QT[m][64:100, :], in_=qaug_d[h, :, :]),
                        W=[("QTa", hp, m)], key="QTa%d_%d" % (hp, m))
                sch.dma("sp", lambda e, m=m: e.dma_start(out=KTown[m][0:64, :], in_=KTo[2 * h + m, :, :]),
                        R=["KTo"], W=[("KTown", hp, m)], key="KTown%d_%d" % (hp, m))
            sch.dma("sp", lambda e: e.dma_start(out=Vop[:, :, :], in_=Vo[:, h, :, :]),
                    R=["Vo"], W=[("Vop", hp)], key="Vop%d" % hp)
            for rg in (3, 2, 1, 0):
                cs = slice(rg * 4096, (rg + 1) * 4096)
                ts_ = slice(rg * 32, (rg + 1) * 32)
                for m in range(2):
                    sch.dma("sp", lambda e, m=m, cs=cs: e.dma_start(out=KT[m][0:64, cs], in_=KTs[2 * h + m, :, cs]),
                            R=["KTs"], W=[("KT", m, rg)], key="KT%d_%d" % (m, rg))
                sch.dma("sp", lambda e, ts_=ts_: e.dma_start(out=Vp[:, ts_, :], in_=Vs[:, h, ts_, :]),
                        R=["Vs"], W=[("Vp", rg)], key="Vp%d" % rg)

            for i in (3, 2, 1, 0):
                qs = slice(i * 512, (i + 1) * 512)
                kb_lo = 0 if W_h is None else max(0, 32 * i - W_h)
                tiles = [("o", jk) for jk in range(4)] + [("g", kb) for kb in range(32 * i + 31, kb_lo - 1, -1)]
                ntile = len(tiles)

                def emit_qk(t):
                    kind, idx = tiles[t]
                    for m in range(2):
                        b = SBANK[m][t % 2]
                        if kind == "g":
                            rgn = idx // 32
                            R_ = 100 if idx >= 32 * i else 68
                            ks = slice(idx * 128, (idx + 1) * 128)
                            sch.op("pe", lambda e, b=b, m=m, R_=R_, ks=ks: e.matmul(
                                banks[b][:, :], lhsT=KT[m][0:R_, ks], rhs=QT[m][0:R_, qs], start=True, stop=True),
                                R=[("KT", m, rgn), ("KTaug", m), ("QT", hp, m), ("QTa", hp, m)], W=[("ps", b)])
                        else:
                            ks = slice((4 * i + idx) * 128, (4 * i + idx + 1) * 128)
                            sch.op("pe", lambda e, b=b, m=m, ks=ks: e.matmul(
                                banks[b][:, :], lhsT=KTown[m][0:68, ks], rhs=QT[m][0:68, qs], start=True, stop=False),
                                R=[("KTown", hp, m), ("KToaug", hp, m), ("QT", hp, m), ("QTa", hp, m)], W=[("ps", b)])
                            sch.op("pe", lambda e, b=b, idx=idx: e.matmul(
                                banks[b][:, :], lhsT=identb, rhs=cmask[:, idx, :], start=False, stop=True),
                                R=["identb", "cmask"], W=[("ps", b)])

                def emit_exp(t):
                    for m in range(2):
                        b = SBANK[m][t % 2]
                        eb = t % 3
                        sch.op("act", lambda e, b=b, m=m, eb=eb: e.activation(out=Eb[m][eb], in_=banks[b][:, :],
                                                                             func=AF.Exp),
                               R=[("ps", b)], W=[("E", m, eb)])

                def emit_av(t):
                    kind, idx = tiles[t]
                    eb = t % 3
                    for j in range(4):
                        for m in range(2):
                            a = j * 2 + m
                            b = ABANK[a // 3]
                            c0 = (a % 3) * VW
                            st = (t == 0 and a % 3 == 0)
                            if kind == "g":
                                rhs = Vp[:, idx, :]
                                vres = [("Vp", idx // 32)]
                            else:
                                rhs = Vop[:, 4 * i + idx, :]
                                vres = [("Vop", hp)]
                            sch.op("pe", lambda e, b=b, c0=c0, m=m, eb=eb, j=j, rhs=rhs, st=st: e.matmul(
                                banks[b][:, c0:c0 + VW], lhsT=Eb[m][eb][:, j * 128:(j + 1) * 128], rhs=rhs,
                                start=st, stop=(t == ntile - 1), skip_group_check=True),
                                R=[("E", m, eb)] + vres, W=[("ps", b)])

                emit_qk(0)
                emit_qk(1)
                for t in range(ntile):
                    if debug and h == 0 and i == 0 and t == ntile - 1:
                        sch.op("dve", lambda e, t=t: e.tensor_copy(out=dbgS, in_=banks[SBANK[0][t % 2]][:, :]),
                               R=[("ps", SBANK[0][t % 2])], W=["dbgS"])
                        sch.dma("sp", lambda e: e.dma_start(out=dbg_s, in_=dbgS), R=["dbgS"], W=["dbgs_o"], key="dbgs")
                        sch.dma("sp", lambda e: e.dma_start(out=dbg_kq[:, 0:512], in_=KTown[0][:, 0:512]),
                                R=[("KTown", 0), ("KToaug", 0)], W=["dbgkq1"], key="dbgkq1")
                        sch.dma("sp", lambda e: e.dma_start(out=dbg_kq[:, 512:1024], in_=QT[0][:, 0:512]),
                                R=[("QT", 0), ("QTa", 0)], W=["dbgkq2"], key="dbgkq2")
                    emit_exp(t)
                    emit_av(t)
                    if t + 2 < ntile:
                        emit_qk(t + 2)

                for bi, b in enumerate(ABANK):
                    na = 3 if bi < 2 else 2
                    src = banks[b][:, 0:na * VW].rearrange("p (a c) -> p a c", a=na)
                    sch.op("dve", lambda e, src=src, bi=bi, na=na: e.tensor_copy(out=accs[:, bi * 3:bi * 3 + na, :],
                                                                                in_=src),
                           R=[("ps", b)], W=["accs"])
                if debug and h == 0 and i == 0:
                    sch.dma("sp", lambda e: e.dma_start(out=dbg_accs, in_=accs.rearrange("p a b -> p (a b)")),
                            R=["accs"], W=["dbgaccs"], key="dbgaccs")
                    sch.dma("sp", lambda e: e.dma_start(out=dbg_e, in_=Eb[0][(ntile - 1) % 3]),
                            R=[("E", 0, (ntile - 1) % 3)], W=["dbge"], key="dbge")
                sch.op("dve", lambda e: e.reciprocal(out=rc, in_=accs[:, :, 128]), R=["accs"], W=["rc"])
                for j in range(4):
                    sch.op("dve", lambda e, j=j: e.tensor_tensor(out=rc[:, 2 * j + 1:2 * j + 2],
                                                                 in0=rc[:, 2 * j + 1:2 * j + 2], in1=neglam,
                                                                 op=ALU.mult), R=["rc", "small"], W=["rc"])
                    sch.op("act", lambda e, j=j: e.activation(out=at, in_=accs[:, 2 * j, 0:128], func=AF.Copy,
                                                              scale=rc[:, 2 * j:2 * j + 1]),
                           R=["accs", "rc"], W=["at"])
                    sch.op("dve", lambda e, j=j: e.scalar_tensor_tensor(
                        out=au, in0=accs[:, 2 * j + 1, 0:128], scalar=rc[:, 2 * j + 1:2 * j + 2], in1=at,
                        op0=ALU.mult, op1=ALU.add), R=["accs", "rc", "at"], W=["au"])
                    sch.op("dve", lambda e, j=j: e.scalar_tensor_tensor(
                        out=junk[:, 0:128], in0=au, scalar=1.0, in1=au, op0=ALU.mult, op1=ALU.mult,
                        accum_out=small[:, 8 + j:9 + j]), R=["au"], W=["junk", ("dss", j)])
                    sch.op("act", lambda e, j=j: e.activation(out=small[:, 8 + j:9 + j], in_=small[:, 8 + j:9 + j],
                                                              func=AF.Sqrt, bias=epst, scale=1.0 / 128),
                           R=[("dss", j), "small"], W=[("dss", j)])
                    sch.op("dve", lambda e, j=j: e.reciprocal(out=small[:, 8 + j:9 + j], in_=small[:, 8 + j:9 + j]),
                           R=[("dss", j)], W=[("dss", j)])
                    sch.op("dve", lambda e, j=j, i=i, h=h: e.scalar_tensor_tensor(
                        out=mix[:, 4 * i + j, h * 128:(h + 1) * 128], in0=au, scalar=small[:, 8 + j:9 + j], in1=gdl,
                        op0=ALU.mult, op1=ALU.mult), R=["au", ("dss", j), "gdl"], W=["mix"])

        if stop == "A":
            raise _Stop()
        sch.barrier()
        ar.reset(persist_mark)

        g2 = ar.alloc([D], F32)
        gf = ar.alloc([D], F32)
        ld(g2, g2_d, "c3", W=["g2"])
        ld(gf, gf_d, "c4", W=["gf"])
        wout = ar.alloc([8, D], BF16)
        for k in range(8):
            sch.dma("pool", lambda e, k=k: e.dma_start(out=wout[:, k, :], in_=w_out[k * 128:(k + 1) * 128, :]),
                    W=["wout"], key="wout%d" % (k % 2))
        NWB = 4
        wbuf = [ar.alloc([8, 512], BF16) for _ in range(NWB)]
        x1 = ar.alloc([4, D], F32)
        h2 = [ar.alloc([D], BF16) for _ in range(2)]
        h2T = ar.alloc([8, 512], BF16)
        mixT = ar.alloc([8, 512], BF16)
        upT = ar.alloc([32, 512], BF16)
        rl = [ar.alloc([512], F32) for _ in range(2)]
        ob = [ar.alloc([D], F32) for _ in range(2)]
        ss2 = [ar.alloc([1], F32) for _ in range(4)]

        NCH = NSLOT * 16
        issued = [0]

        def issue_loads(upto):
            while issued[0] < min(upto, NCH):
                n = issued[0]
                issued[0] += 1
                s_ = n % NWB
                c = n % 16
                if c < 8:
                    sch.dma("sp", lambda e: e.dma_start(out=wbuf[s_], in_=Wup_s[c]), R=["Wup_s"],
                            W=[("wb", s_)], key="wb%d" % s_)
                else:
                    wv_ = wbuf[s_].rearrange("p a b -> p (a b)").rearrange("p (c d) -> p c d", c=4)
                    sch.dma("sp", lambda e: e.dma_start(out=wv_, in_=Wdn_s[c - 8]), R=["Wdn_s"],
                            W=[("wb", s_)], key="wb%d" % s_)

        issue_loads(3)
        mcount = [0]
        tcount = [0]
        for i in range(NSLOT):
            for j in range(4):
                tl = 4 * i + j
                transpose_stage(mix[:, tl, :], "mix", mixT, "mixT", j * 128, "act" if j % 2 == 0 else "dve")
            for j in range(4):
                tl = 4 * i + j
                t = tcount[0]
                tcount[0] += 1
                sch.dma("pool", lambda e: e.dma_start(out=x1[:, j, :], in_=xown[tl * 128:(tl + 1) * 128, :]),
                        W=[("x1", j)], key="x1_%d" % j)
                for half in range(2):
                    b = next_bank()
                    for k in range(8):
                        sch.op("pe", lambda e, k=k: e.matmul(
                            banks[b][:, :], lhsT=mixT[:, k, j * 128:(j + 1) * 128],
                            rhs=wout[:, k, half * 512:(half + 1) * 512], start=(k == 0), stop=(k == 7)),
                            R=["mixT", "wout"], W=[("ps", b)])
                    sch.op("dve", lambda e: e.tensor_tensor(
                        out=x1[:, j, half * 512:(half + 1) * 512], in0=banks[b][:, :],
                        in1=x1[:, j, half * 512:(half + 1) * 512], op=ALU.add), R=[("ps", b), ("x1", j)],
                        W=[("x1", j)])
                hs_ = t % 2
                norm_stage(x1[:, j, :], ("x1", j), g2, "g2", h2[hs_], ("h2", hs_), ss2[t % 4], ("ss2", t % 4))
                transpose_stage(h2[hs_], ("h2", hs_), h2T, "h2T", j * 128, "act" if j % 2 == 1 else "dve")
            for fc8 in range(8):
                n = i * 16 + fc8
                issue_loads(n + 4)
                s_ = n % NWB
                for f4 in range(4):
                    fc = fc8 * 4 + f4
                    b = next_bank()
                    for k in range(8):
                        sch.op("pe", lambda e, k=k: e.matmul(
                            banks[b][:, :], lhsT=wbuf[s_][:, k, f4 * 128:(f4 + 1) * 128], rhs=h2T[:, k, :],
                            start=(k == 0), stop=(k == 7)), R=[("wb", s_), "h2T"], W=[("ps", b)])
                    r = mcount[0] % 2
                    mcount[0] += 1
                    sch.op("act", lambda e: e.activation(out=rl[r], in_=banks[b][:, :], func=AF.Relu),
                           R=[("ps", b)], W=[("rl", r)])
                    sch.op("dve", lambda e: e.tensor_tensor(out=upT[:, fc, :], in0=rl[r], in1=rl[r], op=ALU.mult),
                           R=[("rl", r)], W=["upT"])
            dacc = [[next_bank() for _ in range(2)] for _ in range(4)]
            for fc8 in range(8):
                n = i * 16 + 8 + fc8
                issue_loads(n + 4)
                s_ = n % NWB
                wv = wbuf[s_].rearrange("p a b -> p (a b)").rearrange("p (c d) -> p c d", c=4)
                for j in range(4):
                    for half in range(2):
                        b = dacc[j][half]
                        for f4 in range(4):
                            fc = fc8 * 4 + f4
                            sch.op("pe", lambda e, f4=f4, fc=fc: e.matmul(
                                banks[b][:, :], lhsT=upT[:, fc, j * 128:(j + 1) * 128],
                                rhs=wv[:, f4, half * 512:(half + 1) * 512], start=(fc == 0), stop=(fc == 31),
                                skip_group_check=True), R=[("wb", s_), "upT"], W=[("ps", b)])
            for j in range(4):
                tl = 4 * i + j
                t = tcount[0]
                tcount[0] += 1
                for half in range(2):
                    b = dacc[j][half]
                    sch.op("dve", lambda e: e.tensor_tensor(
                        out=x1[:, j, half * 512:(half + 1) * 512], in0=banks[b][:, :],
                        in1=x1[:, j, half * 512:(half + 1) * 512], op=ALU.add), R=[("ps", b), ("x1", j)],
                        W=[("x1", j)])
                sst = ss2[t % 4]
                ssr = ("ss2", t % 4)
                o_ = t % 2
                sch.op("dve", lambda e: e.scalar_tensor_tensor(
                    out=junk, in0=x1[:, j, :], scalar=1.0, in1=x1[:, j, :], op0=ALU.mult, op1=ALU.mult,
                    accum_out=sst), R=[("x1", j)], W=["junk", ssr])
                sch.op("act", lambda e: e.activation(out=sst, in_=sst, func=AF.Sqrt, bias=epst, scale=1.0 / D),
                       R=[ssr, "small"], W=[ssr])
                sch.op("dve", lambda e: e.reciprocal(out=sst, in_=sst), R=[ssr], W=[ssr])
                sch.op("dve", lambda e: e.scalar_tensor_tensor(
                    out=ob[o_], in0=x1[:, j, :], scalar=sst, in1=gf, op0=ALU.mult, op1=ALU.mult),
                    R=[("x1", j), ssr, "gf"], W=[("ob", o_)])
                sch.dma("pool", lambda e: e.dma_start(out=out_d[tl * 128:(tl + 1) * 128, :], in_=ob[o_]),
                        R=[("ob", o_)], W=[("outd", tl)], key="ob%d" % o_)

        sch.barrier()

    except _Stop:
        pass
    sch.barrier()
    if debug:
        sch.dma("sp", lambda e: e.dma_start(out=dbg_mix.rearrange("(t p) d -> p t d", p=128), in_=mix),
                R=["mix"], W=["dbgmix"], key="dbgmix")
        sch.dma("sp", lambda e: e.dma_start(out=dbg_kt, in_=KTs[:, :, 0:2048]), R=["KTs"], W=["dbgkt"], key="dbgkt")
        sch.dma("sp", lambda e: e.dma_start(out=dbg_v, in_=Vs[:, :, 0:16, :]), R=["Vs"], W=["dbgv"], key="dbgv")
        sch.dma("sp", lambda e: e.dma_start(out=dbg_qt, in_=QTs), R=["QTs"], W=["dbgqt"], key="dbgqt")
        sch.dma("sp", lambda e: e.dma_start(out=dbg_sown, in_=Sown_keep[0].rearrange("p a b -> p (a b)")), R=["Sown"], W=["dbgsown"], key="dbgsown")
        sch.barrier()

    sem_ctx = []

    def semctx(name):
        cx = nc.semaphore(name)
        sem_ctx.append(cx)
        return cx.__enter__()

    sch.finalize(nc, semctx)
    with nc.Block() as block:
        @block.tensor
        def _(e):
            sch.run("pe", e)

        @block.scalar
        def _(e):
            sch.run("act", e)

        @block.vector
        def _(e):
            sch.run("dve", e)

        @block.gpsimd
        def _(e):
            sch.run("pool", e)

        @block.sync
        def _(e):
            sch.run("sp", e)

    for cx in reversed(sem_ctx):
        cx.__exit__(None, None, None)
    for cx in reversed(bank_ctx):
        cx.__exit__(None, None, None)
    ctx_arena.__exit__(None, None, None)
    return nc


def _tables():
    bf = ml_dtypes.bfloat16
    t = np.arange(S)
    kaug = np.zeros((36, S), np.float32)
    kaug[0] = 1.0
    kaug[1] = 1.0
    kaug[2] = t % 128
    kaug[3] = t // 128
    kaug[4 + ((t // 128) % 32), t] = 1.0
    cm = np.zeros((128, 4, 512), np.float32)
    ki = np.arange(128)[:, None]
    qi = np.arange(128)[None, :]
    for jk in range(4):
        for jq in range(4):
            blk = cm[:, jk, jq * 128:(jq + 1) * 128]
            if jq < jk:
                blk[:] = NEG
            elif jq == jk:
                blk[:] = np.where(ki > qi, NEG, 0.0)
    logg = np.log1p(-np.exp2(-5.0 - np.arange(4, dtype=np.float64)))
    dm = np.zeros((128, 4, 512), np.float64)
    qd = np.zeros((128, 4, 512), np.float64)
    kd = np.zeros((128, 16), np.float64)
    r = np.arange(512)
    for h in range(4):
        qd[:, h, :] = np.exp(logg[h] * (r + 1.0))[None, :]
        rel = r[None, :] - np.arange(128)[:, None]
        dm[:, h, :] = np.where(rel >= 0, np.exp(logg[h] * np.maximum(rel, 0)), 0.0)
        for j in range(4):
            kd[:, j * 4 + h] = np.exp(logg[h] * (511.0 - 128 * j - np.arange(128))) * (128.0 ** -0.5)
    return (kaug.astype(bf), cm.astype(bf), dm.astype(np.float32), qd.astype(np.float32),
            kd.astype(np.float32), logg)


def _core_tables(c, logg):
    bf = ml_dtypes.bfloat16
    n = np.arange(OWN)
    tpos = 512 * (8 * (n // 512) + c) + (n % 512)
    qaug = np.zeros((4, 36, OWN), np.float32)
    for h in range(4):
        sl = SLOPES[h]
        qaug[h, 0] = -sl * (tpos % 128)
        qaug[h, 1] = -sl * 128.0 * (tpos // 128)
        qaug[h, 2] = sl
        qaug[h, 3] = sl * 128.0
        for rr in range(32):
            qaug[h, 4 + rr] = NEG if rr >= 4 * c else 0.0
    kaugo = np.zeros((4, OWN), np.float32)
    kaugo[0] = 1.0
    kaugo[1] = 1.0
    kaugo[2] = tpos % 128
    kaugo[3] = tpos // 128
    rc = np.zeros((NCOEF,), np.float64)
    for h in range(4):
        rc[h] = np.exp(logg[h] * 512.0)
        rc[4 + h] = np.exp(logg[h] * 512.0 * c)
        for cp in range(8):
            rc[8 + cp * 4 + h] = np.exp(logg[h] * 512.0 * (c - 1 - cp)) if cp < c else 0.0
    rcoef = np.broadcast_to(rc.astype(np.float32)[None, :], (128, NCOEF)).copy()
    return qaug.astype(bf), kaugo.astype(bf), rcoef


_CACHE = {}


def kernel(x, norm1_g, w_in, lambda_q1, lambda_k1, lambda_q2, lambda_k2, diff_norm_g, ret_norm_g,
           w_out, norm2_g, w_up, w_down, final_norm_g, _debug=False, _stop=None):
    f32 = np.float32
    x2 = np.ascontiguousarray(np.asarray(x, f32).reshape(S, D))
    bc = lambda v, n: np.ascontiguousarray(np.broadcast_to(np.asarray(v, f32).reshape(1, n), (128, n)))
    lamv = np.concatenate([bc(lambda_q1, 64), bc(lambda_k1, 64), bc(lambda_q2, 64), bc(lambda_k2, 64)], axis=1)
    kaug, cm, dm, qd, kd, logg = _tables()
    common = {
        "xall": x2,
        "w_in": np.ascontiguousarray(np.asarray(w_in, f32).reshape(D, INW)),
        "w_out": np.ascontiguousarray(np.asarray(w_out, f32).reshape(D, D)),
        "w_up": np.ascontiguousarray(np.asarray(w_up, f32).reshape(D, DFF)),
        "w_down": np.ascontiguousarray(np.asarray(w_down, f32).reshape(DFF, D)),
        "g1": bc(norm1_g, D), "g2": bc(norm2_g, D), "gf": bc(final_norm_g, D),
        "gdiff": bc(diff_norm_g, 128), "gret": bc(ret_norm_g, 512), "lamv": np.ascontiguousarray(lamv),
        "kaug": kaug, "cmask": cm, "identf": np.eye(128, dtype=f32),
        "identb": np.eye(128, dtype=f32).astype(ml_dtypes.bfloat16),
        "dm": dm, "qdec": qd, "kdecp": kd,
    }
    in_maps = []
    x4 = x2.reshape(NSLOT, NCORES, 512, D)
    for c in range(NCORES):
        qaug, kaugo, rcoef = _core_tables(c, logg)
        m = dict(common)
        m["xown"] = np.ascontiguousarray(x4[:, c].reshape(OWN, D))
        m["qaug"] = qaug
        m["kaugo"] = kaugo
        m["rcoef"] = rcoef
        in_maps.append(m)
    key = (bool(_debug), _stop)
    if key not in _CACHE:
        _CACHE[key] = build_program(debug=bool(_debug), stop=_stop)
    nc = _CACHE[key]
    res = run_bass_kernel_spmd(nc, in_maps, core_ids=list(range(NCORES)))
    out = np.zeros((NSLOT, NCORES, 512, D), f32)
    for c in range(NCORES):
        out[:, c] = np.asarray(res.results[c]["out"], f32).reshape(NSLOT, 512, D)
    full = out.reshape(1, S, D)
    if _debug:
        mixd = np.zeros((NSLOT, NCORES, 512, D), f32)
        for c in range(NCORES):
            mixd[:, c] = np.asarray(res.results[c]["dbg_mix"]).astype(f32).reshape(NSLOT, 512, D)
        dbg = {"mix": mixd.reshape(S, D), "res": res.results}
        return full, dbg
    return full
```
